# Optimizing a Trainium2 kernel written in Bass

```python
import math
import jax, jax.numpy as jnp
from jax import lax
import numpy as np

D_MODEL = 1024
BATCH = 16
SEQ = 256
DEPTH = 4
DEC_BATCH = 8
DEC_SEQ = 1024
PAST_LEN = 256

GRID_W = 64
N_HEADS = 16
HEAD_DIM = 64
KV_HEADS = 4
Q_PER_KV = N_HEADS // KV_HEADS
QKV_DIM = (N_HEADS + 2 * KV_HEADS) * HEAD_DIM
WINDOW = 128
BLOCK = 128
ROPE_BASE = 10000.0
ROT_AXIS_DIM = HEAD_DIM // 2
ATTN_SCALE = HEAD_DIM ** -0.5
HY_ORDER = 2
HY_DIRS = 2
FILTER_BANDS = 16
FILTER_EMB = 1 + 2 * FILTER_BANDS
FILTER_HIDDEN = 64
FILTER_FREQ = 1.0
DECAY_FAST_PCT = 0.3
DECAY_SLOW_PCT = 1.5
DECAY_TARGET = 1e-2
D_FF = 2816
N_MOD = 9
LN_EPS = 1e-5
N_ATTN_LAYERS = (DEPTH + 1) // 2
N_HYENA_LAYERS = DEPTH // 2
DEEPNORM_ALPHA = (2 * DEPTH) ** 0.25
DEEPNORM_BETA = (8 * DEPTH) ** -0.25
NEG_INF = -1e30

kernel_name = "hybrid_swa_hyena_macaron_diffusion_step"


def layer_norm(x, g, b):
    xf = x.astype(jnp.float32)
    mu = jnp.mean(xf, -1, keepdims=True)
    var = jnp.mean(jnp.square(xf - mu), -1, keepdims=True)
    return ((xf - mu) * lax.rsqrt(var + LN_EPS) * g + b).astype(x.dtype)


def modulate(x, shift, scale):
    return x * (1 + scale[:, None]) + shift[:, None]


def swiglu(h, w1, w2):
    g, u = jnp.split(h @ w1, 2, axis=-1)
    return (jax.nn.silu(g) * u) @ w2


def grid_positions(L):
    rows = L // GRID_W
    r = jnp.repeat(jnp.arange(rows), GRID_W).astype(jnp.float32)
    c = jnp.tile(jnp.arange(GRID_W), rows).astype(jnp.float32)
    return r, c


def rope_axis(x, pos):
    half = ROT_AXIS_DIM // 2
    inv = ROPE_BASE ** (-jnp.arange(half, dtype=jnp.float32) / half)
    ang = pos[:, None] * inv[None]
    cos = jnp.cos(ang)[None, :, None, :]
    sin = jnp.sin(ang)[None, :, None, :]
    xf = x.astype(jnp.float32)
    x1, x2 = xf[..., :half], xf[..., half:]
    return jnp.concatenate([x1 * cos - x2 * sin, x2 * cos + x1 * sin], -1).astype(x.dtype)


def rope_2d(x):
    r, c = grid_positions(x.shape[1])
    return jnp.concatenate([rope_axis(x[..., :ROT_AXIS_DIM], r), rope_axis(x[..., ROT_AXIS_DIM:], c)], -1)


def split_qkv(h, w_qkv):
    B, L, _ = h.shape
    qkv = h @ w_qkv
    nq, nk = N_HEADS * HEAD_DIM, KV_HEADS * HEAD_DIM
    q = qkv[..., :nq].reshape(B, L, N_HEADS, HEAD_DIM)
    k = qkv[..., nq:nq + nk].reshape(B, L, KV_HEADS, HEAD_DIM)
    v = qkv[..., nq + nk:].reshape(B, L, KV_HEADS, HEAD_DIM)
    return q, k, v


def sink_logits(sink, B, Lq):
    s = sink.astype(jnp.float32).reshape(KV_HEADS, Q_PER_KV)[None, :, :, None, None]
    return jnp.broadcast_to(s, (B, KV_HEADS, Q_PER_KV, Lq, 1))


def attn_context(h, w_qkv, w_o, sink):
    B, L, _ = h.shape
    q, k, v = split_qkv(h, w_qkv)

    def block(b):
        qb = lax.dynamic_slice_in_dim(q, b * BLOCK, BLOCK, axis=1).reshape(B, BLOCK, KV_HEADS, Q_PER_KV, HEAD_DIM)
        s = jnp.einsum('bqkgd,bckd->bkgqc', qb, k).astype(jnp.float32) * ATTN_SCALE
        p = jax.nn.softmax(jnp.concatenate([s, sink_logits(sink, B, BLOCK)], -1), axis=-1)[..., :L]
        o = jnp.einsum('bkgqc,bckd->bqkgd', p.astype(v.dtype), v)
        return o.reshape(B, BLOCK, N_HEADS * HEAD_DIM)

    o = lax.map(block, jnp.arange(L // BLOCK))
    o = jnp.transpose(o, (1, 0, 2, 3)).reshape(B, L, N_HEADS * HEAD_DIM)
    return o @ w_o, k, v


def attn_latent(h, k_ctx, v_ctx, w_qkv, w_o, sink):
    B, L, _ = h.shape
    Lc = k_ctx.shape[1]
    S = BLOCK + 2 * WINDOW
    q, k, v = split_qkv(h, w_qkv)
    q, k = rope_2d(q), rope_2d(k)
    pad = ((0, 0), (WINDOW, WINDOW), (0, 0), (0, 0))
    kp, vp = jnp.pad(k, pad), jnp.pad(v, pad)

    def block(b):
        start = b * BLOCK
        qb = lax.dynamic_slice_in_dim(q, start, BLOCK, axis=1).reshape(B, BLOCK, KV_HEADS, Q_PER_KV, HEAD_DIM)
        kb = lax.dynamic_slice_in_dim(kp, start, S, axis=1)
        vb = lax.dynamic_slice_in_dim(vp, start, S, axis=1)
        qi = start + jnp.arange(BLOCK)
        kj = start - WINDOW + jnp.arange(S)
        valid = (jnp.abs(qi[:, None] - kj[None, :]) <= WINDOW) & (kj >= 0)[None] & (kj < L)[None]
        s_loc = jnp.einsum('bqkgd,bskd->bkgqs', qb, kb).astype(jnp.float32) * ATTN_SCALE
        s_loc = jnp.where(valid, s_loc, NEG_INF)
        s_ctx = jnp.einsum('bqkgd,bckd->bkgqc', qb, k_ctx).astype(jnp.float32) * ATTN_SCALE
        p = jax.nn.softmax(jnp.concatenate([s_loc, s_ctx, sink_logits(sink, B, BLOCK)], -1), axis=-1)
        o = (jnp.einsum('bkgqs,bskd->bqkgd', p[..., :S].astype(vb.dtype), vb)
             + jnp.einsum('bkgqc,bckd->bqkgd', p[..., S:S + Lc].astype(v_ctx.dtype), v_ctx))
        return o.reshape(B, BLOCK, N_HEADS * HEAD_DIM)

    o = lax.map(block, jnp.arange(L // BLOCK))
    o = jnp.transpose(o, (1, 0, 2, 3)).reshape(B, L, N_HEADS * HEAD_DIM)
    return o @ w_o


def short_conv(u, w, b):
    up = jnp.pad(u, ((0, 0), (1, 1), (0, 0)))
    return up[:, :-2] * w[0] + up[:, 1:-1] * w[1] + up[:, 2:] * w[2] + b


def hyena_filter_spectrum(L, f_w1, f_b1, f_w2, f_b2, f_w3):
    t = jnp.arange(L, dtype=jnp.float32) / L
    bands = jnp.arange(1, FILTER_BANDS + 1, dtype=jnp.float32)
    ph = 2 * jnp.pi * t[:, None] * bands[None]
    feats = jnp.concatenate([t[:, None], jnp.sin(ph), jnp.cos(ph)], -1)
    a = jnp.sin(FILTER_FREQ * (feats @ f_w1.astype(jnp.float32) + f_b1.astype(jnp.float32)))
    a = jnp.sin(FILTER_FREQ * (a @ f_w2.astype(jnp.float32) + f_b2.astype(jnp.float32)))
    hf = (a @ f_w3.astype(jnp.float32)).reshape(L, HY_ORDER, HY_DIRS, D_MODEL)
    max_decay = math.log(DECAY_TARGET) / DECAY_FAST_PCT
    min_decay = math.log(DECAY_TARGET) / DECAY_SLOW_PCT
    deltas = jnp.abs(jnp.linspace(min_decay, max_decay, D_MODEL, dtype=jnp.float32))
    hf = hf * jnp.exp(-t[:, None] * deltas[None])[:, None, None, :]
    fwd, bwd = hf[:, :, 0], hf[:, :, 1]
    full = jnp.concatenate([fwd, jnp.zeros((1, HY_ORDER, D_MODEL), jnp.float32), bwd[:0:-1]], axis=0)
    full = full / (jnp.sum(jnp.abs(full), axis=0, keepdims=True) + 1e-6)
    return jnp.fft.rfft(full, axis=0)


def long_conv(z, spec, d):
    L = z.shape[1]
    zf32 = z.astype(jnp.float32)
    zf = jnp.fft.rfft(zf32, n=2 * L, axis=1)
    y = jnp.fft.irfft(zf * spec[None], n=2 * L, axis=1)[:, :L]
    return (y + zf32 * d.astype(jnp.float32)).astype(z.dtype)


def hyena(h, w_in, conv_w, conv_b, f_w1, f_b1, f_w2, f_b2, f_w3, hy_d, w_out):
    L = h.shape[1]
    u = short_conv(h @ w_in, conv_w, conv_b)
    v, x1, x2 = jnp.split(u, 3, axis=-1)
    spec = hyena_filter_spectrum(L, f_w1, f_b1, f_w2, f_b2, f_w3)
    z = x1 * long_conv(v, spec[:, 0], hy_d[0])
    z = x2 * long_conv(z, spec[:, 1], hy_d[1])
    return z @ w_out


def trunk(x, cond, P, ctx_k=None, ctx_v=None):
    new_k, new_v = [], []
    for i in range(DEPTH):
        mod = (jax.nn.silu(cond) @ P['ada_w'][i] + P['ada_b'][i]).reshape(cond.shape[0], N_MOD, D_MODEL)
        h = modulate(x, mod[:, 0], mod[:, 1])
        f = swiglu(h, P['ffn_w1'][i, 0], P['ffn_w2'][i, 0])
        x = layer_norm(DEEPNORM_ALPHA * x + 0.5 * mod[:, 2][:, None] * f, P['ln_g'][i, 0], P['ln_b'][i, 0])
        h = modulate(x, mod[:, 3], mod[:, 4])
        j = i // 2
        if i % 2 == 0:
            if ctx_k is None:
                out, k, v = attn_context(h, P['attn_w_qkv'][j], P['attn_w_o'][j], P['attn_sink'][j])
                new_k.append(k)
                new_v.append(v)
            else:
                out = attn_latent(h, ctx_k[:, j], ctx_v[:, j], P['attn_w_qkv'][j], P['attn_w_o'][j], P['attn_sink'][j])
        else:
            out = hyena(h, P['hy_w_in'][j], P['hy_conv_w'][j], P['hy_conv_b'][j], P['hy_f_w1'][j], P['hy_f_b1'][j],
                        P['hy_f_w2'][j], P['hy_f_b2'][j], P['hy_f_w3'][j], P['hy_d'][j], P['hy_w_out'][j])
        x = layer_norm(DEEPNORM_ALPHA * x + mod[:, 5][:, None] * out, P['ln_g'][i, 1], P['ln_b'][i, 1])
        h = modulate(x, mod[:, 6], mod[:, 7])
        f = swiglu(h, P['ffn_w1'][i, 1], P['ffn_w2'][i, 1])
        x = layer_norm(DEEPNORM_ALPHA * x + 0.5 * mod[:, 8][:, None] * f, P['ln_g'][i, 2], P['ln_b'][i, 2])
    return x, new_k, new_v


def setup_inputs(seed: int = 0) -> dict:
    key = jax.random.key(seed)
    ks = jax.random.split(key, 26)
    f32 = jnp.float32
    nrm = lambda k, shape, s: jax.random.normal(k, shape, f32) * s
    D, A, Hy = D_MODEL, N_ATTN_LAYERS, N_HYENA_LAYERS
    return {
        "x_prompt": nrm(ks[0], (BATCH, SEQ, D), 1.0),
        "x_sample": nrm(ks[1], (DEC_BATCH, DEC_SEQ, D), 1.0),
        "cache_k": nrm(ks[2], (DEC_BATCH, A, PAST_LEN, KV_HEADS, HEAD_DIM), 1.0),
        "cache_v": nrm(ks[3], (DEC_BATCH, A, PAST_LEN, KV_HEADS, HEAD_DIM), 1.0),
        "c": nrm(ks[4], (DEC_BATCH, D), 1.0),
        "c_ctx": nrm(ks[5], (D,), 1.0),
        "ada_w": nrm(ks[6], (DEPTH, D, N_MOD * D), D ** -0.5),
        "ada_b": nrm(ks[7], (DEPTH, N_MOD * D), 0.02),
        "ln_g": 1.0 + nrm(ks[8], (DEPTH, 3, D), 0.02),
        "ln_b": nrm(ks[9], (DEPTH, 3, D), 0.02),
        "ffn_w1": nrm(ks[10], (DEPTH, 2, D, 2 * D_FF), D ** -0.5),
        "ffn_w2": nrm(ks[11], (DEPTH, 2, D_FF, D), D_FF ** -0.5 * DEEPNORM_BETA),
        "attn_w_qkv": nrm(ks[12], (A, D, QKV_DIM), D ** -0.5),
        "attn_w_o": nrm(ks[13], (A, N_HEADS * HEAD_DIM, D), (N_HEADS * HEAD_DIM) ** -0.5 * DEEPNORM_BETA),
        "attn_sink": nrm(ks[14], (A, N_HEADS), 0.5),
        "hy_w_in": nrm(ks[15], (Hy, D, 3 * D), D ** -0.5),
        "hy_conv_w": nrm(ks[16], (Hy, 3, 3 * D), 3 ** -0.5),
        "hy_conv_b": nrm(ks[17], (Hy, 3 * D), 0.02),
        "hy_f_w1": nrm(ks[18], (Hy, FILTER_EMB, FILTER_HIDDEN), FILTER_EMB ** -0.5 * 2.0),
        "hy_f_b1": nrm(ks[19], (Hy, FILTER_HIDDEN), 0.5),
        "hy_f_w2": nrm(ks[20], (Hy, FILTER_HIDDEN, FILTER_HIDDEN), FILTER_HIDDEN ** -0.5 * 2.0),
        "hy_f_b2": nrm(ks[21], (Hy, FILTER_HIDDEN), 0.5),
        "hy_f_w3": nrm(ks[22], (Hy, FILTER_HIDDEN, HY_ORDER * HY_DIRS * D), FILTER_HIDDEN ** -0.5),
        "hy_d": nrm(ks[23], (Hy, HY_ORDER, D), 0.1),
        "hy_w_out": nrm(ks[24], (Hy, D, D), D ** -0.5 * DEEPNORM_BETA),
    }


def reference(x_prompt, x_sample, cache_k, cache_v, c, c_ctx, ada_w, ada_b, ln_g, ln_b, ffn_w1, ffn_w2,
              attn_w_qkv, attn_w_o, attn_sink, hy_w_in, hy_conv_w, hy_conv_b, hy_f_w1, hy_f_b1, hy_f_w2,
              hy_f_b2, hy_f_w3, hy_d, hy_w_out):
    P = {
        'ada_w': ada_w, 'ada_b': ada_b, 'ln_g': ln_g, 'ln_b': ln_b,
        'ffn_w1': ffn_w1, 'ffn_w2': ffn_w2,
        'attn_w_qkv': attn_w_qkv, 'attn_w_o': attn_w_o, 'attn_sink': attn_sink,
        'hy_w_in': hy_w_in, 'hy_conv_w': hy_conv_w, 'hy_conv_b': hy_conv_b,
        'hy_f_w1': hy_f_w1, 'hy_f_b1': hy_f_b1, 'hy_f_w2': hy_f_w2, 'hy_f_b2': hy_f_b2,
        'hy_f_w3': hy_f_w3, 'hy_d': hy_d, 'hy_w_out': hy_w_out,
    }
    y_prompt, ks_list, vs_list = trunk(x_prompt, c_ctx[None], P)
    new_cache_k = jnp.stack(ks_list, axis=1)
    new_cache_v = jnp.stack(vs_list, axis=1)
    y_sample, _, _ = trunk(x_sample, c, P, cache_k, cache_v)
    return (y_prompt, y_sample, new_cache_k, new_cache_v)
```

```python
import math
from collections import defaultdict
from contextlib import ExitStack

import numpy as np
import concourse.bass as bass
import concourse.mybir as mybir
from concourse.bass_utils import run_bass_kernel_spmd

F32 = mybir.dt.float32
BF16 = mybir.dt.bfloat16
AF = mybir.ActivationFunctionType
ALU = mybir.AluOpType

D = 1024
KC = 8
DFF = 2816
NJ = 22
DEPTH = 4
NTOK = 1536
NB = 3
TB = 512
ALPHA = (2 * DEPTH) ** 0.25
LN_EPS = 1e-5
NCORES = 8
SLOT_COLS = 4096
NSLOTS = 3


ACOLS = 18944
HZ2 = 0
HVZ = 12288
HTAPS = 18432
HG = 26624
HA2 = 34816
HW3 = 36096

AQ = 0
AK = 8 * NTOK
AKC = AK + 2 * NTOK
AVP = AKC + 512
APT = AVP + 14 * 512
NPT = 5
ART = (APT + NPT * 512) // 2
AROPE = ART + 1024
ASTG = AROPE + 2048


class View:
    __slots__ = ("ap", "keys")

    def __init__(self, ap, keys):
        self.ap = ap
        self.keys = keys


class Tile:
    def __init__(self, name, handle, ncols, gran):
        self.name = name
        self.h = handle
        self.ncols = ncols
        self.gran = gran

    def _keys(self, b0, b1, p0, p1):
        halves = (0, 1)
        if not self.name.startswith("ps"):
            if p0 >= 64:
                halves = (1,)
            elif p1 <= 64:
                halves = (0,)
        return [(self.name, k, h) for k in range(b0 // self.gran, (b1 - 1) // self.gran + 1) for h in halves]

    def v(self, c0, c1, p0=0, p1=128):
        assert 0 <= c0 < c1 <= self.ncols, (self.name, c0, c1, self.ncols)
        return View(self.h[p0:p1, c0:c1], self._keys(c0, c1, p0, p1))

    def vb(self, c0, c1, p0=0, p1=128):
        assert c0 % 2 == 0 and c1 % 2 == 0
        b0, b1 = c0 // 2, c1 // 2
        assert 0 <= b0 < b1 <= self.ncols, (self.name, c0, c1, self.ncols)
        return View(self.h[p0:p1, b0:b1].bitcast(BF16), self._keys(b0, b1, p0, p1))


class Op:
    __slots__ = ("eng", "fn", "reads", "writes", "kind", "slot", "slot_count", "pos", "signal",
                 "sigcount", "waits", "tag")


class Sched:
    def __init__(self, nc):
        self.nc = nc
        self.ops = []
        self.slot_counts = defaultdict(int)
        self.slot_last = {}
        self.tag = ""
        self.names = {}

    def op(self, eng, fn, reads=(), writes=()):
        o = Op()
        o.eng, o.fn, o.kind, o.slot, o.slot_count = eng, fn, "c", None, 0
        o.tag = self.tag
        o.reads = [k for v in reads for k in v.keys]
        o.writes = [k for v in writes for k in v.keys]
        o.writes += [k for k in o.reads if k[0].startswith("ps")]
        o.signal = False
        self.ops.append(o)
        return o

    def dma(self, queue, out, in_, slot, reads=(), writes=()):
        o = Op()
        o.eng, o.kind, o.slot = queue, "d", slot
        o.tag = self.tag
        self.slot_counts[slot] += 1
        o.slot_count = self.slot_counts[slot]
        oap = out.ap if isinstance(out, View) else out
        iap = in_.ap if isinstance(in_, View) else in_
        o.fn = lambda e, oap=oap, iap=iap: e.dma_start(out=oap, in_=iap)
        o.reads = [k for v in reads for k in v.keys]
        o.writes = [k for v in writes for k in v.keys]
        if isinstance(out, View):
            o.writes += out.keys
        if isinstance(in_, View):
            o.reads += in_.keys
        o.signal = False
        self.ops.append(o)
        return o

    def finalize(self):
        ops = self.ops
        last_w = {}
        readers = {}
        cnt = defaultdict(int)
        waited = defaultdict(lambda: -1)
        slot_prev = {}
        for i, op in enumerate(ops):
            op.pos = cnt[op.eng]
            cnt[op.eng] += 1
            deps = set()
            for k in op.reads:
                j = last_w.get(k)
                if j is not None:
                    deps.add(j)
            for k in op.writes:
                j = last_w.get(k)
                if j is not None:
                    deps.add(j)
                deps.update(readers.get(k, ()))
            if op.kind == "d" and op.slot in slot_prev:
                deps.add(slot_prev[op.slot])
            op.waits = []
            for j in sorted(deps):
                src = ops[j]
                if src.kind == "d":
                    key = (op.eng, "slot", src.slot)
                    if waited[key] >= src.slot_count:
                        continue
                    waited[key] = src.slot_count
                    op.waits.append(j)
                else:
                    if src.eng == op.eng and op.kind == "c":
                        if op.eng == "pe":
                            continue
                    key = (op.eng, src.eng)
                    if waited[key] >= src.pos:
                        continue
                    waited[key] = src.pos
                    op.waits.append(j)
                    src.signal = True
            for k in op.reads:
                readers.setdefault(k, []).append(i)
            for k in op.writes:
                last_w[k] = i
                readers[k] = []
            if op.kind == "d":
                slot_prev[op.slot] = i
        sc = defaultdict(int)
        for op in ops:
            if op.kind == "c" and op.signal:
                sc[op.eng] += 1
            op.sigcount = sc[op.eng]

    def emit(self, stack):
        nc = self.nc
        self.finalize()
        ops = self.ops
        engs = ["pe", "act", "dve", "pool", "sp"]
        sems = {e: stack.enter_context(nc.semaphore("s_" + e)) for e in engs}
        slot_sems = {s: stack.enter_context(nc.semaphore("d_" + str(s))) for s in self.slot_counts}
        streams = {e: [o for o in ops if o.eng == e] for e in engs}
        block = stack.enter_context(nc.Block())

        def run(e, ename):
            for o in streams[ename]:
                for j in o.waits:
                    src = ops[j]
                    if src.kind == "d":
                        e.wait_ge(slot_sems[src.slot], 16 * src.slot_count)
                    else:
                        e.wait_ge(sems[src.eng], src.sigcount)
                ins = o.fn(e)
                try:
                    self.names[ins.ins.name] = o.tag
                except Exception:
                    pass
                if o.kind == "d":
                    ins.then_inc(slot_sems[o.slot], 16)
                elif o.signal:
                    ins.then_inc(sems[ename], 1)
            if ename == "sp":
                for s, n in self.slot_counts.items():
                    e.wait_ge(slot_sems[s], 16 * n)

        @block.tensor
        def _(e):
            run(e, "pe")

        @block.scalar
        def _(e):
            run(e, "act")

        @block.vector
        def _(e):
            run(e, "dve")

        @block.gpsimd
        def _(e):
            run(e, "pool")

        @block.sync
        def _(e):
            run(e, "sp")


class StopBuild(Exception):
    pass


class Builder:
    def ckpt(self, name):
        if self.cfg.get("stop_at") == name:
            raise StopBuild()

    def __init__(self, cfg):
        self.cfg = cfg
        self.nc = bass.Bass("TRN2", target_bir_lowering=False)
        self.stack = ExitStack()
        self.S = Sched(self.nc)
        self.wcount = 0
        self.tmp_i = 0
        self.ps_i = 0

    def dram_in(self, name, shape, dt=F32):
        return self.nc.dram_tensor(name, list(shape), dt, kind="ExternalInput").ap()

    def dram_out(self, name, shape, dt=F32):
        return self.nc.dram_tensor(name, list(shape), dt, kind="ExternalOutput").ap()

    def sb(self, name, ncols, dt, gran=512):
        h = self.stack.enter_context(self.nc.sbuf_tensor(name, [128, ncols], dt))
        return Tile(name, h, ncols, gran)

    def psum_banks(self):
        self.PS = []
        for i in range(8):
            h = self.stack.enter_context(self.nc.psum_tensor("ps%d" % i, [128, 512], F32))
            self.PS.append(Tile("ps%d" % i, h, 512, 512))

    def ps_next(self, pool=(0, 1, 2, 3)):
        b = pool[self.ps_i % len(pool)]
        self.ps_i += 1
        return self.PS[b]

    def tmp_next(self):
        i = self.tmp_i % (self.NTMP - 2)
        self.tmp_i += 1
        return self.TMP.v(i * 512, (i + 1) * 512)

    def wload(self, dram_ap, ncols):
        s = self.wcount % NSLOTS
        self.wcount += 1
        v = self.WR.v(s * SLOT_COLS, s * SLOT_COLS + ncols)
        self.S.dma("pool", v, dram_ap, slot="w%d" % s)
        return s * SLOT_COLS

    def mm(self, ps_view, lhsT, rhs, start, stop):
        self.S.op("pe", lambda e, o=ps_view.ap, l=lhsT.ap, r=rhs.ap, st=start, sp=stop:
                  e.matmul(o, l, r, start=st, stop=sp), reads=[lhsT, rhs], writes=[ps_view])

    def act(self, out, in_, func, bias=None, scale=None, extra_reads=()):
        kw = {}
        if bias is not None:
            kw["bias"] = bias.ap if isinstance(bias, View) else bias
        if scale is not None:
            kw["scale"] = scale.ap if isinstance(scale, View) else scale
        rd = [in_] + [x for x in (bias, scale) if isinstance(x, View)] + list(extra_reads)
        self.S.op("act", lambda e, o=out.ap, i=in_.ap, f=func, kw=kw: e.activation(o, i, f, **kw),
                  reads=rd, writes=[out])

    def tt(self, out, in0, in1, op, eng="dve"):
        self.S.op(eng, lambda e, o=out.ap, a=in0.ap, b=in1.ap, op=op: e.tensor_tensor(o, a, b, op),
                  reads=[in0, in1], writes=[out])

    def ts(self, out, in0, s1, s2, op0, op1=None, eng="dve"):
        rd = [in0] + [x for x in (s1, s2) if isinstance(x, View)]
        a1 = s1.ap if isinstance(s1, View) else s1
        a2 = s2.ap if isinstance(s2, View) else s2
        if op1 is None:
            self.S.op(eng, lambda e, o=out.ap, a=in0.ap: e.tensor_scalar(o, a, a1, None, op0),
                      reads=rd, writes=[out])
        else:
            self.S.op(eng, lambda e, o=out.ap, a=in0.ap: e.tensor_scalar(o, a, a1, a2, op0, op1),
                      reads=rd, writes=[out])

    def recip(self, v):
        self.act(v, v, AF.Ln)
        self.act(v, v, AF.Exp, scale=-1.0)

    def stt(self, out, in0, scalar, in1, op0, op1, eng="dve"):
        rd = [in0, in1] + ([scalar] if isinstance(scalar, View) else [])
        sc = scalar.ap if isinstance(scalar, View) else scalar
        self.S.op(eng, lambda e, o=out.ap, a=in0.ap, b=in1.ap: e.scalar_tensor_tensor(o, a, sc, b, op0, op1),
                  reads=rd, writes=[out])

    def build(self):
        cfg = self.cfg
        nlayers = cfg.get("nlayers", DEPTH)
        xT_d = self.dram_in("xT", [D, NTOK])
        cond_d = self.dram_in("condT", [128, KC * 2])
        adaw_d = self.dram_in("adaw", [DEPTH, 18, 128, KC * 512])
        adab_d = self.dram_in("adab", [128, DEPTH * 72])
        lng_d = self.dram_in("lng", [128, DEPTH * 3 * KC])
        lnb_d = self.dram_in("lnb", [128, DEPTH * 3 * KC])
        w1_d = self.dram_in("w1", [DEPTH * 2, 11, 128, KC * 512])
        w2_d = self.dram_in("w2", [DEPTH * 2, KC, 128, NJ * 128])
        yT_d = self.dram_out("yT", [D, NTOK])
        wqkv_d = self.dram_in("wqkv", [2, 6, 128, KC * 512])
        wo_d = self.dram_in("wo", [2, 2, 128, KC * 512])
        rope_d = self.dram_in("rope", [128, 2048])
        masks_d = self.dram_in("masks", [128, 1024])
        ident_d = self.dram_in("ident", [128, 128])
        sinkb_d = self.dram_in("sinkb", [128, 32])
        ck_d = self.dram_in("ck", [2, 256, 256])
        cv_d = self.dram_in("cv", [2, 256, 256])
        nk_d = self.dram_out("nk", [2, 512, 256])
        nv_d = self.dram_out("nv", [2, 512, 256])
        self.attn_d = (wqkv_d, wo_d, rope_d, ck_d, cv_d, nk_d, nv_d)
        self.hy_d = dict(
            win=self.dram_in("hwin", [2, 6, 128, KC * 512]),
            wout=self.dram_in("hwout", [2, 2, 128, KC * 512]),
            dft1024=self.dram_in("dft1024", [8, 128, 4096]),
            dft256=self.dram_in("dft256", [128, 2048]),
            dec1024=self.dram_in("dec1024", [8, 128, 1024]),
            dec256=self.dram_in("dec256", [2, 128, 1024]),
            feats=self.dram_in("feats", [128, 1280]),
            w3p=self.dram_in("w3p", [2, 2, 2, 64, 1024]),
            hyd=self.dram_in("hyd", [128, 2 * 2 * 1024]),
            convp=self.dram_in("convp", [128, 2 * 24 * 4]),
            fw1=self.dram_in("fw1", [128, 128]),
            fw2=self.dram_in("fw2", [128, 128]),
            fb=self.dram_in("fb", [128, 4]),
        )

        self.X = self.sb("X", KC * NTOK, F32)
        self.H = self.sb("H", KC * NTOK, BF16)
        self.A = self.sb("A", ACOLS, F32, gran=64)
        self.WR = self.sb("WR", NSLOTS * SLOT_COLS, BF16, gran=SLOT_COLS)
        self.NTMP = 6
        self.TMP = self.sb("TMP", self.NTMP * 512, F32)
        self.LNA = self.sb("LNA", NB * 512, F32)
        self.LNB = self.sb("LNB", NB * 512, F32)
        self.ONES = self.sb("ONES", 128, F32, gran=128)
        self.ONESB = self.sb("ONESB", 128, BF16, gran=128)
        self.stat_i = 0
        self.stat_base = 33792
        self.EPS = self.sb("EPS", 1, F32, gran=1)
        self.COND = self.sb("COND", KC * 2, F32, gran=16)
        self.SC = self.sb("SC", KC * 2, BF16, gran=16)
        self.MOD = self.sb("MOD", 2 * 72 * 2, F32, gran=2)
        self.MOD1 = self.sb("MOD1", 2 * 72 * 2, F32, gran=2)
        self.mb = 0
        self.bgq = []
        self.ADAB = self.sb("ADAB", DEPTH * 72, F32, gran=72)
        self.LNG = self.sb("LNG", DEPTH * 3 * KC, F32, gran=KC)
        self.LNBT = self.sb("LNBT", DEPTH * 3 * KC, F32, gran=KC)
        self.FG = self.sb("FG", KC * 2, F32, gran=16)
        self.FB = self.sb("FB", KC * 2, F32, gran=16)
        self.IDENT = self.sb("IDENT", 128, F32, gran=128)
        self.ONESK = self.sb("ONESK", 128, BF16, gran=128)
        self.MASKS = self.sb("MASKS", 1024, BF16, gran=512)
        self.IDB = self.sb("IDB", 128, BF16, gran=128)
        self.ESINK = self.sb("ESINK", 16, F32, gran=16)
        self.SINKB = self.sb("SINKB", 32, F32, gran=32)
        self.CONVP = self.sb("CONVP", 2 * 24 * 4, F32, gran=4)
        self.FW1 = self.sb("FW1", 128, F32, gran=128)
        self.FW2 = self.sb("FW2", 128, F32, gran=128)
        self.FBT = self.sb("FBT", 4, F32, gran=4)
        self.FBP = self.sb("FBP", 4, F32, gran=4)
        self.NEGPI = self.sb("NEGPI", 1, F32, gran=1)
        self.ABSB = self.sb("ABSB", 1024, BF16, gran=512)
        self.RNC = self.sb("RNC", 4, F32, gran=4)
        self.abs_i = 0
        self.pt_i = 0
        self.rt_i = 0
        self.stg_i = 0
        self.psum_banks()
        S = self.S
        S.dma("sp", self.IDENT.v(0, 128), ident_d, slot="c4")
        S.dma("sp", self.SINKB.v(0, 32), sinkb_d, slot="c5")
        S.dma("pool", self.MASKS.v(0, 1024), masks_d, slot="c6")
        S.dma("pool", self.IDB.v(0, 128), ident_d, slot="c11")
        S.op("dve", lambda e: e.memset(self.ONESK.h[:, :], 1.0), writes=[self.ONESK.v(0, 128)])
        S.op("dve", lambda e: e.memset(self.NEGPI.h[:, :], -math.pi), writes=[self.NEGPI.v(0, 1)])
        S.dma("sp", self.CONVP.v(0, 192), self.hy_d["convp"], slot="c7")
        S.dma("sp", self.FW1.v(0, 128), self.hy_d["fw1"], slot="c8")
        S.dma("sp", self.FW2.v(0, 128), self.hy_d["fw2"], slot="c9")
        S.dma("sp", self.FBT.v(0, 4), self.hy_d["fb"], slot="c10")
        self.ts(self.FBP.v(0, 4), self.FBT.v(0, 4), 17.0 * math.pi, None, ALU.add)

        for kc in range(KC):
            S.dma("sp", self.X.v(kc * NTOK, (kc + 1) * NTOK), xT_d[kc * 128:(kc + 1) * 128, :], slot="x%d" % kc)
        S.dma("sp", self.COND.v(0, KC * 2), cond_d, slot="c0")
        S.dma("sp", self.ADAB.v(0, DEPTH * 72), adab_d, slot="c1")
        S.dma("sp", self.LNG.v(0, DEPTH * 3 * KC), lng_d, slot="c2")
        S.dma("sp", self.LNBT.v(0, DEPTH * 3 * KC), lnb_d, slot="c3")
        S.op("dve", lambda e: e.memset(self.ONES.h[:, :], 1.0 / D), writes=[self.ONES.v(0, 128)])
        S.op("dve", lambda e: e.memset(self.ONESB.h[:, :], 1.0 / D), writes=[self.ONESB.v(0, 128)])
        S.op("dve", lambda e: e.memset(self.EPS.h[:, :], LN_EPS), writes=[self.EPS.v(0, 1)])
        self.act(self.SC.v(0, KC * 2), self.COND.v(0, KC * 2), AF.Silu)

        try:
            self.layers(nlayers, adaw_d, w1_d, w2_d)
        except StopBuild:
            pass

        for kc in range(KC):
            S.dma("sp", yT_d[kc * 128:(kc + 1) * 128, :], self.X.v(kc * NTOK, (kc + 1) * NTOK), slot="y%d" % kc)
        S.emit(self.stack)
        return self.nc

    def layers(self, nlayers, adaw_d, w1_d, w2_d):
        cfg = self.cfg
        self.adaw_d = adaw_d
        for li in range(nlayers):
            self.S.tag = "L%d ada" % li
            if li == 0:
                for pc in range(6):
                    self.ada_piece(0, pc)
                self.ada_finish(0, ks=(1, 2))
                self.ada_enqueue(0, first=6)
            else:
                self.bg(100)
            self.mb = (li % 2) * 144
            if li == 0:
                self.mod_in(li, 0)
            self.S.tag = "L%d ffn0" % li
            self.ffn(li, 0, w1_d, w2_d)
            self.S.tag = "L%d ln0" % li
            if li == 0:
                self.bg(100)
            self.ln(li, 0, next_k=3)
            self.S.tag = "L%d mixer" % li
            mx = cfg.get("mixers", "all")
            if li % 2 == 0 and mx in ("all", "attn"):
                self.attention(li)
            elif li % 2 == 1 and mx in ("all", "hyena"):
                self.hyena(li)
            else:
                self.ffn_dummy_mixer(li)
            self.S.tag = "L%d ffn1" % li
            if li + 1 < nlayers:
                self.ada_enqueue(li + 1)
            self.ffn(li, 1, w1_d, w2_d)
            self.S.tag = "L%d ln2" % li
            if li + 1 < nlayers:
                self.bg(100)
                self.ln(li, 2, next_k=0, mb=((li + 1) % 2) * 144)
            else:
                self.ln(li, 2, next_k=None)

    def ada_piece(self, li, pc, base=None):
        mb = (li % 2) * 144
        adaw_d = self.adaw_d
        if base is None:
            base = self.wload(adaw_d[li, pc], KC * 512)
        for mi in range(4):
            m = pc * 4 + mi
            ps = self.ps_next(pool=(0, 1))
            for kc in range(KC):
                self.mm(ps.v(0, 2), self.WR.v(base + kc * 512 + mi * 128, base + kc * 512 + (mi + 1) * 128),
                        self.SC.v(kc * 2, kc * 2 + 2), kc == 0, kc == KC - 1)
            self.ts(self.MOD.v(mb + m * 2, mb + m * 2 + 2), ps.v(0, 2),
                    self.ADAB.v(li * 72 + m, li * 72 + m + 1), None, ALU.add)

    def ada_finish(self, li, ks=(1, 2, 4, 5, 7, 8)):
        mb = (li % 2) * 144
        for k in ks:
            src = self.MOD.v(mb + k * 16, mb + k * 16 + 16)
            dst = self.MOD1.v(mb + k * 16, mb + k * 16 + 16)
            if k in (1, 4, 7):
                self.ts(dst, src, 1.0, None, ALU.add)
            elif k in (2, 8):
                self.ts(dst, src, 0.5, None, ALU.mult)
            else:
                self.ts(dst, src, 1.0, None, ALU.mult)

    def ada_enqueue(self, li, first=0):
        for pc in range(first, 18):
            self.bgq.append(lambda li=li, pc=pc: self.ada_piece(li, pc))
        if first == 0:
            self.bgq.append(lambda li=li: self.ada_finish(li))
        else:
            self.bgq.append(lambda li=li: self.ada_finish(li, ks=(4, 5, 7, 8)))

    def bg(self, n=1):
        for _ in range(n):
            if not self.bgq:
                return
            tag = self.S.tag
            self.S.tag = "bg ada"
            self.bgq.pop(0)()
            self.S.tag = tag

    def modv(self, c0, c1):
        return self.MOD.v(self.mb + c0, self.mb + c1)

    def mod1v(self, c0, c1):
        return self.MOD1.v(self.mb + c0, self.mb + c1)

    def tokgroups(self):
        return [(0, 0, 512), (1, 512, 1024), (1, 1024, 1536)]

    def mod_in(self, li, k):
        for kc in range(KC):
            for (n, t0, t1) in self.tokgroups():
                c = (k * 8 + kc) * 2 + n
                c1 = ((k + 1) * 8 + kc) * 2 + n
                self.act(self.H.v(kc * NTOK + t0, kc * NTOK + t1), self.X.v(kc * NTOK + t0, kc * NTOK + t1),
                         AF.Identity, bias=self.modv(c, c + 1), scale=self.mod1v(c1, c1 + 1))

    def ffn(self, li, s, w1_d, w2_d):
        self.stat_base = 33792
        kg = 2 if s == 0 else 8
        fi = li * 2 + s
        def unit(base, pc, b):
            for pr in range(2):
                j = pc * 2 + pr
                psg = self.ps_next()
                psu = self.ps_next()
                for half, ps in ((0, psg), (1, psu)):
                    off = pr * 256 + half * 128
                    for kc in range(KC):
                        self.mm(ps.v(0, 512), self.WR.v(base + kc * 512 + off, base + kc * 512 + off + 128),
                                self.H.v(kc * NTOK + b * TB, kc * NTOK + (b + 1) * TB), kc == 0, kc == KC - 1)
                t = self.tmp_next()
                self.act(t, psg.v(0, 512), AF.Silu)
                self.tt(self.A.vb(j * NTOK + b * TB, j * NTOK + (b + 1) * TB), t, psu.v(0, 512), ALU.mult)

        GP = NSLOTS
        bases = [self.wload(w1_d[fi, pc], KC * 512) for pc in range(GP)]
        for b in range(NB):
            for pc in range(GP):
                unit(bases[pc], pc, b)
        self.bg(GP)
        for pc in range(GP, 11):
            base = self.wload(w1_d[fi, pc], KC * 512)
            for b in range(NB):
                unit(base, pc, b)
            self.bg()
        self.S.tag = self.S.tag + " p2"
        for m in range(KC):
            base = self.wload(w2_d[fi, m], NJ * 128)
            for b in range(NB):
                ps = self.ps_next(pool=(0, 1))
                for j in range(NJ):
                    self.mm(ps.v(0, 512), self.WR.v(base + j * 128, base + (j + 1) * 128),
                            self.A.vb(j * NTOK + b * TB, j * NTOK + (b + 1) * TB), j == 0, j == NJ - 1)
                n = 0 if b == 0 else 1
                c = (kg * 8 + m) * 2 + n
                self.resid_and_stats(ps, m, b, self.mod1v(c, c + 1))
            self.bg()

    def resid_and_stats(self, ps, m, b, gate):
        t = self.tmp_next()
        self.act(t, ps.v(0, 512), AF.Identity, scale=gate)
        xv = self.X.v(m * NTOK + b * TB, m * NTOK + (b + 1) * TB)
        self.stt(xv, xv, ALPHA, t, ALU.mult, ALU.add)
        t2 = self.tmp_next()
        self.tt(t2, xv, xv, ALU.mult)
        sb = self.stat_base + (self.stat_i % 2) * 2048
        self.stat_i += 1
        yh, yl, sh, sl = (self.A.vb(sb + i * 512, sb + (i + 1) * 512) for i in range(4))
        self.act(yh, xv, AF.Identity)
        self.tt(yl, xv, yh, ALU.subtract)
        self.act(sh, t2, AF.Identity)
        self.tt(sl, t2, sh, ALU.subtract)
        self.flush_stats()
        self.pending_stats = (yh, yl, sh, sl, m, b)

    def flush_stats(self):
        p = getattr(self, "pending_stats", None)
        if p is None:
            return
        yh, yl, sh, sl, m, b = p
        self.mm(self.PS[2 + b].v(0, 512), self.ONESB.v(0, 128), yh, m == 0, False)
        self.mm(self.PS[2 + b].v(0, 512), self.ONESB.v(0, 128), yl, False, m == KC - 1)
        self.mm(self.PS[5 + b].v(0, 512), self.ONESB.v(0, 128), sh, m == 0, False)
        self.mm(self.PS[5 + b].v(0, 512), self.ONESB.v(0, 128), sl, False, m == KC - 1)
        self.pending_stats = None

    def ffn_dummy_mixer(self, li):
        for m in range(KC):
            for b in range(NB):
                xv = self.X.v(m * NTOK + b * TB, m * NTOK + (b + 1) * TB)
                self.ts(xv, xv, ALPHA, None, ALU.mult)
                t2 = self.tmp_next()
                self.tt(t2, xv, xv, ALU.mult)
                self.mm(self.PS[2 + b].v(0, 512), self.ONES.v(0, 128), xv, m == 0, m == KC - 1)
                self.mm(self.PS[5 + b].v(0, 512), self.ONES.v(0, 128), t2, m == 0, m == KC - 1)
        self.ln(li, 1, next_k=6)

    def ln(self, li, s, next_k, mb=None):
        self.flush_stats()
        if mb is None:
            mb = self.mb
        gi = (li * 3 + s) * KC
        if next_k is not None:
            for kc in range(KC):
                cs = mb + ((next_k + 1) * 8 + kc) * 2
                ch = mb + (next_k * 8 + kc) * 2
                self.ts(self.FG.v(kc * 2, kc * 2 + 2), self.MOD1.v(cs, cs + 2),
                        self.LNG.v(gi + kc, gi + kc + 1), None, ALU.mult)
                self.stt(self.FB.v(kc * 2, kc * 2 + 2), self.MOD1.v(cs, cs + 2),
                         self.LNBT.v(gi + kc, gi + kc + 1), self.MOD.v(ch, ch + 2), ALU.mult, ALU.add)
        for b in range(NB):
            mean = self.PS[2 + b].v(0, 512)
            e2 = self.PS[5 + b].v(0, 512)
            la = self.LNA.v(b * 512, (b + 1) * 512)
            lb = self.LNB.v(b * 512, (b + 1) * 512)
            t = self.tmp_next()
            self.act(t, mean, AF.Square)
            self.tt(la, e2, t, ALU.subtract)
            self.act(la, la, AF.Ln, bias=self.EPS.v(0, 1))
            self.act(la, la, AF.Exp, scale=-0.5)
            self.stt(lb, mean, -1.0, la, ALU.mult, ALU.mult)
        for (n, t0, t1) in ((0, 0, 512), (1, 512, 1536)):
            la = self.LNA.v(t0, t1)
            lb = self.LNB.v(t0, t1)
            for kc in range(KC):
                xv = self.X.v(kc * NTOK + t0, kc * NTOK + t1)
                self.tt(xv, xv, la, ALU.mult)
                self.tt(xv, xv, lb, ALU.add)
                if next_k is not None:
                    self.act(self.H.v(kc * NTOK + t0, kc * NTOK + t1), xv, AF.Identity,
                             bias=self.FB.v(kc * 2 + n, kc * 2 + n + 1),
                             scale=self.FG.v(kc * 2 + n, kc * 2 + n + 1))
                self.act(xv, xv, AF.Identity, bias=self.LNBT.v(gi + kc, gi + kc + 1),
                         scale=self.LNG.v(gi + kc, gi + kc + 1))

    def proj_rope(self, bw, offw, bp, offp, b, dest):
        A = self.A
        ps = self.ps_next(pool=(0, 1, 2, 3, 4, 5, 6, 7))
        for kc in range(KC):
            self.mm(ps.v(0, 512), self.WR.v(bw + kc * 512 + offw, bw + kc * 512 + offw + 128),
                    self.H.v(kc * NTOK + b * TB, kc * NTOK + (b + 1) * TB), kc == 0, kc == KC - 1)
        if b == 0:
            self.act(dest, ps.v(0, 512), AF.Identity)
            return
        ps2 = self.ps_next(pool=(0, 1, 2, 3, 4, 5, 6, 7))
        for kc in range(KC):
            self.mm(ps2.v(0, 512), self.WR.v(bp + kc * 512 + offp, bp + kc * 512 + offp + 128),
                    self.H.v(kc * NTOK + b * TB, kc * NTOK + (b + 1) * TB), kc == 0, kc == KC - 1)
        t0 = (b - 1) * 512
        cosv = A.v(AROPE + t0, AROPE + t0 + 512)
        sinv = A.v(AROPE + 1024 + t0, AROPE + 1024 + t0 + 512)
        t1 = self.tmp_next()
        t2 = self.tmp_next()
        self.tt(t1, ps.v(0, 512), cosv, ALU.mult)
        self.tt(t2, ps2.v(0, 512), sinv, ALU.mult)
        self.tt(dest, t1, t2, ALU.add)

    def copy_any(self, out, in_):
        self.cp_i = getattr(self, "cp_i", 0) + 1
        if self.cp_i % 2:
            self.act(out, in_, AF.Identity)
        else:
            self.S.op("dve", lambda e, o=out.ap, i=in_.ap: e.tensor_copy(o, i), reads=[in_], writes=[out])

    def attention(self, li):
        wqkv_d, wo_d, rope_d, ck_d, cv_d, nk_d, nv_d = self.attn_d
        j = li // 2
        A = self.A
        S = self.S
        S.dma("sp", A.v(AROPE, AROPE + 2048), rope_d, slot="rope")
        self.act(self.ESINK.v(0, 16), self.SINKB.v(j * 16, j * 16 + 16), AF.Exp)
        vpall = A.vb(AVP, AVP + 14 * 512)
        S.op("dve", lambda e, o=vpall.ap: e.memset(o, 1.0), writes=[vpall])
        stgk = ASTG + 0
        stgv = ASTG + 512
        for tl in range(2):
            S.dma("sp", A.v(stgk + tl * 256, stgk + (tl + 1) * 256), ck_d[j, tl * 128:(tl + 1) * 128, :], slot="ck%d" % tl)
            S.dma("sp", A.v(stgv + tl * 256, stgv + (tl + 1) * 256), cv_d[j, tl * 128:(tl + 1) * 128, :], slot="cv%d" % tl)
        for tl in range(2):
            for g in range(4):
                self.copy_any(A.vb(AVP + (12 + tl) * 512 + g * 128 + (g % 2) * 64, AVP + (12 + tl) * 512 + g * 128 + (g % 2) * 64 + 64),
                              A.v(stgv + tl * 256 + g * 64, stgv + tl * 256 + g * 64 + 64))
            for c in range(2):
                ps = self.ps_next()
                src = A.v(stgk + tl * 256 + c * 128, stgk + tl * 256 + (c + 1) * 128)
                self.S.op("pe", lambda e, o=ps.v(0, 128).ap, i=src.ap, idn=self.IDENT.h[:, :]: e.transpose(o, i, idn),
                          reads=[src, self.IDENT.v(0, 128)], writes=[ps.v(0, 128)])
                self.copy_any(A.vb(AKC + c * 256 + tl * 128, AKC + c * 256 + (tl + 1) * 128), ps.v(0, 128))
        stage = self.cfg.get("attn_stage", 9)
        if stage <= 1:
            return self.ffn_dummy_mixer(li)
        for half in range(2):
            bq = self.wload(wqkv_d[j, half], KC * 512)
            bp = self.wload(wqkv_d[j, 2 + half], KC * 512)
            for b in range(NB):
                for ci in range(4):
                    c = half * 4 + ci
                    self.proj_rope(bq, ci * 128, bp, ci * 128, b,
                                   A.vb(AQ + c * NTOK + b * TB, AQ + c * NTOK + (b + 1) * TB))
        bk = self.wload(wqkv_d[j, 4], KC * 512)
        for b in range(NB):
            for c in range(2):
                self.proj_rope(bk, c * 128, bk, 256 + c * 128, b,
                               A.vb(AK + c * NTOK + b * TB, AK + c * NTOK + (b + 1) * TB))
        if stage <= 2:
            return self.ffn_dummy_mixer(li)
        bkv = self.wload(wqkv_d[j, 5], KC * 512)
        for tt_ in range(12):
            prompt = tt_ < 4
            ps = self.ps_next()
            c0 = 0 if prompt else 256
            for kc in range(KC):
                self.mm(ps.v(c0, 512), self.H.v(kc * NTOK + tt_ * 128, kc * NTOK + (tt_ + 1) * 128),
                        self.WR.v(bkv + kc * 512 + c0, bkv + kc * 512 + 512), kc == 0, kc == KC - 1)
            for g in range(4):
                if self.cfg.get("s3_nocopy"):
                    break
                self.copy_any(A.vb(AVP + tt_ * 512 + g * 128 + (g % 2) * 64, AVP + tt_ * 512 + g * 128 + (g % 2) * 64 + 64),
                              ps.v(256 + g * 64, 256 + g * 64 + 64))
            if prompt and not self.cfg.get("s3_nodma"):
                si = 2
                stg = A.v(ASTG + si * 512, ASTG + (si + 1) * 512)
                self.act(stg, ps.v(0, 512), AF.Identity)
                S.dma("sp", nk_d[j, tt_ * 128:(tt_ + 1) * 128, :], A.v(ASTG + si * 512, ASTG + si * 512 + 256), slot="ok")
                S.dma("sp", nv_d[j, tt_ * 128:(tt_ + 1) * 128, :], A.v(ASTG + si * 512 + 256, ASTG + si * 512 + 512), slot="ov")
        if stage <= 3:
            return self.ffn_dummy_mixer(li)
        self.S.tag = "L%d attn core" % li
        self.attn_units = []
        for sq in range(2):
            for i in range(2):
                for g in range(4):
                    self.attn_core(sq * 2 + i, g, [("loc", sq * 2, None), ("loc", sq * 2 + 1, None)])
        for i in range(8):
            for g in range(4):
                kts = []
                if i > 0:
                    kts.append(("loc", 4 + i - 1, 0))
                kts.append(("loc", 4 + i, None))
                if i < 7:
                    kts.append(("loc", 4 + i + 1, 1))
                kts += [("ctx", 0, None), ("ctx", 1, None)]
                self.attn_core(4 + i, g, kts)
        self.attn_flush()
        if stage <= 4:
            return self.ffn_dummy_mixer(li)
        self.stat_base = APT
        self.S.tag = "L%d attn oproj" % li
        for pi in range(2):
            bw = self.wload(wo_d[j, pi], KC * 512)
            for b in range(NB):
                for mi in range(4):
                    m = pi * 4 + mi
                    ps = self.ps_next(pool=(0, 1))
                    for c in range(8):
                        self.mm(ps.v(0, 512), self.WR.v(bw + c * 512 + mi * 128, bw + c * 512 + (mi + 1) * 128),
                                A.vb(AQ + c * NTOK + b * TB, AQ + c * NTOK + (b + 1) * TB), c == 0, c == 7)
                    n = 0 if b == 0 else 1
                    cidx = (5 * 8 + m) * 2 + n
                    self.resid_and_stats(ps, m, b, self.mod1v(cidx, cidx + 1))
        self.S.tag = "L%d ln1" % li
        self.ln(li, 1, next_k=6)

    def attn_core(self, qt, g, keytiles):
        self.attn_units.append((qt, g, keytiles))

    def attn_flush(self):
        A = self.A
        units = self.attn_units
        self.attn_units = []
        steps = []
        for ui, (qt, g, kts) in enumerate(units):
            for ki, kt in enumerate(kts):
                steps.append((ui, ki, len(kts), kt))
        unit_ps = {}
        state = {}

        def stage_a(si):
            ui, ki, n, (kind, idx, mask) = steps[si]
            qt, g, _ = units[ui]
            half = g % 2
            p0, p1 = half * 64, half * 64 + 64
            kc_ = g // 2
            qc0 = (g // 2) * 4
            if ki == 0:
                unit_ps[ui] = (self.PS[2 + ui % 6], None)
            psS = self.ps_next(pool=(0, 1))
            if kind == "loc":
                kview = A.vb(AK + kc_ * NTOK + idx * 128, AK + kc_ * NTOK + (idx + 1) * 128, p0, p1)
            else:
                kview = A.vb(AKC + kc_ * 256 + idx * 128, AKC + kc_ * 256 + (idx + 1) * 128, p0, p1)
            if mask is not None:
                self.mm(psS.v(0, 512), self.IDB.v(0, 128), self.MASKS.v(mask * 512, (mask + 1) * 512), True, False)
            qkeys = []
            for jj in range(4):
                qkeys += A.vb(AQ + (qc0 + jj) * NTOK + qt * 128, AQ + (qc0 + jj) * NTOK + (qt + 1) * 128, p0, p1).keys
            qap = A.h[p0:p1, AQ // 2:(AQ + 8 * NTOK) // 2].bitcast(BF16).rearrange("p (c t) -> p c t", c=8)[
                :, qc0:qc0 + 4, qt * 128:(qt + 1) * 128]
            self.mm(psS.v(0, 512), kview, View(qap, qkeys), mask is None, True)
            pt = A.vb(APT + (self.pt_i % NPT) * 512, APT + (self.pt_i % NPT + 1) * 512)
            self.pt_i += 1
            self.act(pt, psS.v(0, 512), AF.Exp, scale=0.125)
            state[si] = pt

        def stage_c(si):
            ui, ki, n, (kind, idx, mask) = steps[si]
            qt, g, _ = units[ui]
            psO, psL = unit_ps[ui]
            pt = state.pop(si)
            vbase = AVP + idx * 512 if kind == "loc" else AVP + (12 + idx) * 512
            self.mm(psO.v(0, 512), A.vb(vbase + g * 128, vbase + (g + 1) * 128), pt, ki == 0, ki == n - 1)
            if ki == n - 1:
                self.attn_finalize(qt, g, psO, psL)

        ns = len(steps)
        DEPTH_A = 1
        for si in range(ns + DEPTH_A):
            if si < ns:
                stage_a(si)
            if si >= DEPTH_A:
                stage_c(si - DEPTH_A)

    def attn_finalize(self, qt, g, psO, psL):
        A = self.A
        half = g % 2
        p0, p1 = half * 64, half * 64 + 64
        qc0 = (g // 2) * 4
        rt = A.v(ART + (self.rt_i % 2) * 512, ART + (self.rt_i % 2 + 1) * 512, p0, p1)
        self.rt_i += 1
        rt3 = rt.ap.rearrange("p (h q) -> p h q", h=4)
        q0, q1 = (64, 128) if half == 0 else (0, 64)
        l3 = psO.v(0, 512, q0, q1).ap.rearrange("p (h q) -> p h q", h=4)
        es = self.ESINK.v(g * 4, g * 4 + 4, p0, p1)
        es3 = es.ap.unsqueeze(2).broadcast_to([64, 4, 128])
        self.S.op("dve", lambda e, o=rt3, a=l3, b=es3: e.tensor_tensor(o, a, b, ALU.add),
                  reads=[psO.v(0, 512), es], writes=[rt])
        self.recip(rt)
        keys = []
        for jj in range(4):
            keys += A.vb(AQ + (qc0 + jj) * NTOK + qt * 128, AQ + (qc0 + jj) * NTOK + (qt + 1) * 128, p0, p1).keys
        oap = A.h[p0:p1, AQ // 2:(AQ + 8 * NTOK) // 2].bitcast(BF16).rearrange("p (c t) -> p c t", c=8)[
            :, qc0:qc0 + 4, qt * 128:(qt + 1) * 128]
        ov = View(oap, keys)
        o3 = psO.v(0, 512, p0, p1).ap.rearrange("p (h q) -> p h q", h=4)
        self.S.op("dve", lambda e, o=oap, a=o3, b=rt3: e.tensor_tensor(o, a, b, ALU.mult),
                  reads=[psO.v(0, 512), rt], writes=[ov])

    def pool_ts(self, out, in0, s1, s2, op0, op1):
        rd = [in0] + [x for x in (s1, s2) if isinstance(x, View)]
        a1 = s1.ap if isinstance(s1, View) else s1
        a2 = s2.ap if isinstance(s2, View) else s2
        self.S.op("pool", lambda e, o=out.ap, a=in0.ap: e.tensor_scalar(o, a, a1, a2, op0, op1), reads=rd, writes=[out])

    def sin_reduce(self, zs, npart):
        MAGIC = 12582912.0
        t = self.tmp_next()
        tv = View(t.ap[0:npart, 0:zs.ap.shape[1]], t.keys)
        self.ts(tv, zs, 1.0 / (2.0 * math.pi), MAGIC, ALU.mult, ALU.add)
        self.ts(tv, tv, MAGIC, -2.0 * math.pi, ALU.subtract, ALU.mult)
        self.tt(zs, zs, tv, ALU.add)
        self.ts(zs, zs, -math.pi, math.pi, ALU.max, ALU.min)

    def hy_filter_mlp(self, j):
        A, S = self.A, self.S
        S.dma("sp", self.LNA.v(0, 1280, 0, 33), self.hy_d["feats"][0:33, :], slot="feats")
        for (c0, c1) in ((0, 512), (512, 1024), (1024, 1280)):
            n = c1 - c0
            ps = self.ps_next()
            self.mm(ps.v(0, n, 0, 64), self.FW1.v(j * 64, j * 64 + 64, 0, 33), self.LNA.v(c0, c1, 0, 33), True, True)
            zs = self.LNB.v(c0, c1, 0, 64)
            self.act(zs, ps.v(0, n, 0, 64), AF.Identity, bias=self.FBT.v(j * 2, j * 2 + 1, 0, 64))
            self.sin_reduce(zs, 64)
            self.act(zs, zs, AF.Sin)
        for (c0, c1) in ((0, 512), (512, 1024), (1024, 1280)):
            n = c1 - c0
            ps = self.ps_next()
            self.mm(ps.v(0, n, 0, 64), self.FW2.v(j * 64, j * 64 + 64, 0, 64), self.LNB.v(c0, c1, 0, 64), True, True)
            zs = self.LNA.v(c0, c1, 0, 64)
            self.act(zs, ps.v(0, n, 0, 64), AF.Identity, bias=self.FBT.v(j * 2 + 1, j * 2 + 2, 0, 64))
            self.sin_reduce(zs, 64)
            self.act(A.vb(HA2 + c0, HA2 + c1, 0, 64), zs, AF.Sin)

    def uview(self, cc, c0, c1):
        t = self.LNA if cc < 2 else self.LNB
        base = (cc % 2) * 512
        return t.v(base + c0, base + c1)

    def hy_inproj(self, j, chunk, b, wbase, cc):
        woff = (chunk % 4) * 128
        ps = self.ps_next(pool=(0, 1, 2, 3, 4, 5, 6))
        for kc in range(KC):
            self.mm(ps.v(0, 512), self.WR.v(wbase + kc * 512 + woff, wbase + kc * 512 + woff + 128),
                    self.H.v(kc * NTOK + b * TB, kc * NTOK + (b + 1) * TB), kc == 0, kc == KC - 1)
        psh = None
        if b > 0:
            tokh = 1024 if b == 1 else 1023
            psh = self.ps_next(pool=(0, 1, 2, 3, 4, 5, 6))
            for kc in range(KC):
                self.mm(psh.v(0, 1), self.WR.v(wbase + kc * 512 + woff, wbase + kc * 512 + woff + 128),
                        self.H.v(kc * NTOK + tokh, kc * NTOK + tokh + 1), kc == 0, kc == KC - 1)
        cp = (j * 24 + chunk) * 4
        w0, w1, w2, bb = (self.CONVP.v(cp + i, cp + i + 1) for i in range(4))
        self.act(self.uview(cc, 0, 512), ps.v(0, 512), AF.Identity, bias=bb, scale=w1)
        segs = [(0, 256), (256, 512)] if b == 0 else [(0, 512)]
        for (s0, e0) in segs:
            self.stt(self.uview(cc, s0 + 1, e0), ps.v(s0, e0 - 1), w0, self.uview(cc, s0 + 1, e0), ALU.mult, ALU.add)
            self.stt(self.uview(cc, s0, e0 - 1), ps.v(s0 + 1, e0), w2, self.uview(cc, s0, e0 - 1), ALU.mult, ALU.add)
        if b == 1:
            self.stt(self.uview(cc, 511, 512), psh.v(0, 1), w2, self.uview(cc, 511, 512), ALU.mult, ALU.add)
        if b == 2:
            self.stt(self.uview(cc, 0, 1), psh.v(0, 1), w0, self.uview(cc, 0, 1), ALU.mult, ALU.add)

    def hy_to_tokmajor(self, src, b, cc):
        A = self.A
        ps = self.ps_next(pool=(0, 1, 2, 3, 4, 5, 6))
        for i in range(4):
            sv = View(src.ap[:, i * 128:(i + 1) * 128], src.keys)
            self.S.op("pe", lambda e, o=ps.v(i * 128, (i + 1) * 128).ap, s_=sv.ap, idn=self.IDENT.h[:, :]: e.transpose(o, s_, idn),
                      reads=[sv, self.IDENT.v(0, 128)], writes=[ps.v(i * 128, (i + 1) * 128)])
        keys = []
        for i in range(4):
            keys += A.vb(HVZ + (b * 4 + i) * 512 + cc * 128, HVZ + (b * 4 + i) * 512 + (cc + 1) * 128).keys
        dap = A.h[:, HVZ // 2:(HVZ + 12 * 512) // 2].bitcast(BF16).rearrange("p (k c) -> p k c", k=12)[
            :, b * 4:b * 4 + 4, cc * 128:(cc + 1) * 128]
        dst = View(dap, keys)
        p3 = ps.v(0, 512).ap.rearrange("p (k c) -> p k c", k=4)
        self.S.op("act", lambda e, o=dap, i_=p3: e.activation(o, i_, AF.Identity), reads=[ps.v(0, 512)], writes=[dst])

    def hy_taps(self, j, order, cb, L):
        A, S = self.A, self.S
        KT = L // 128
        a2off = HA2 + (0 if L == 1024 else 1024)
        dec_d = self.hy_d["dec1024"] if L == 1024 else self.hy_d["dec256"]
        w3v = A.vb(HW3, HW3 + 1024, 0, 64)
        S.dma("pool", w3v, self.hy_d["w3p"][j, order, cb], slot="w3p")
        psN = self.PS[7]
        for kt in range(KT):
            dec = self.tmp_next()
            S.dma("sp", dec, dec_d[kt, :, cb * 512:(cb + 1) * 512], slot="dec%d" % (self.tmp_i % (self.NTMP - 2)))
            lhs = A.vb(a2off + kt * 128, a2off + (kt + 1) * 128, 0, 64)
            psf = self.ps_next(pool=(0, 1, 2, 3, 4, 5, 6))
            psb = self.ps_next(pool=(0, 1, 2, 3, 4, 5, 6))
            self.mm(psf.v(0, 512), lhs, A.vb(HW3, HW3 + 512, 0, 64), True, True)
            self.mm(psb.v(0, 512), lhs, A.vb(HW3 + 512, HW3 + 1024, 0, 64), True, True)
            tf = self.tmp_next()
            tb_ = self.tmp_next()
            self.tt(tf, psf.v(0, 512), dec, ALU.mult)
            self.tt(tb_, psb.v(0, 512), dec, ALU.mult)
            if kt == 0:
                z = View(tb_.ap[0:1, :], tb_.keys)
                S.op("dve", lambda e, o=z.ap: e.memset(o, 0.0), reads=[tb_], writes=[tb_])
            self.tt(A.vb(HTAPS + kt * 512, HTAPS + (kt + 1) * 512), tf, tb_, ALU.add)
            self.tt(A.vb(HTAPS + (KT + kt) * 512, HTAPS + (KT + kt + 1) * 512), tf, tb_, ALU.subtract)
            for ti, tsrc in enumerate((tf, tb_)):
                ab = self.ABSB.v((self.abs_i % 2) * 512, (self.abs_i % 2 + 1) * 512)
                self.abs_i += 1
                self.act(ab, tsrc, AF.Abs)
                self.mm(psN.v(0, 512), self.ONESK.v(0, 128), ab, kt == 0 and ti == 0, kt == KT - 1 and ti == 1)
        rn = self.TMP.v((self.NTMP - 2) * 512, (self.NTMP - 1) * 512)
        dn = self.TMP.v((self.NTMP - 1) * 512, self.NTMP * 512)
        self.ts(rn, psN.v(0, 512), 1e-6, None, ALU.add)
        S.dma("sp", dn, self.hy_d["hyd"][:, (j * 2 + order) * 1024 + cb * 512:(j * 2 + order) * 1024 + (cb + 1) * 512],
              slot="dbc")
        self.tt(dn, dn, rn, ALU.mult)
        self.recip(rn)
        ps = self.ps_next(pool=(0, 1, 2, 3, 4, 5, 6))
        for cc in range(4):
            src = View(rn.ap[0:1, cc * 128:(cc + 1) * 128], rn.keys)
            S.op("pe", lambda e, o=ps.v(cc, cc + 1).ap, s_=src.ap, idn=self.IDENT.h[0:1, 0:1]: e.transpose(o, s_, idn),
                 reads=[src, self.IDENT.v(0, 128)], writes=[ps.v(cc, cc + 1)])
        self.act(self.RNC.v(0, 4), ps.v(0, 4), AF.Identity)
        return rn, dn

    def hy_conv_group(self, j, order, cb, grp):
        A, S = self.A, self.S
        L = 256 if grp == "p" else 1024
        KT = L // 128
        self.S.tag = "taps o%d cb%d %s" % (order, cb, grp)
        rn, dn = self.hy_taps(j, order, cb, L)
        self.ckpt("taps_%d_%d_%s" % (order, cb, grp))
        self.S.tag = "fwd o%d cb%d %s" % (order, cb, grp)
        nseq = 2 if grp == "p" else 1
        pool7 = (0, 1, 2, 3, 4, 5, 6)
        if grp == "s":
            fblocks = [(0, 4), (4, 8)]
        else:
            fblocks = [(0, 2)]
            bdft = self.wload(self.hy_d["dft256"], 2048)
        for fbi, (m0, m1) in enumerate(fblocks):
            if grp == "s":
                bC = self.wload(self.hy_d["dft1024"][0 * 2 + fbi], 4096)
                bS = self.wload(self.hy_d["dft1024"][1 * 2 + fbi], 4096)
                cstride = 512
            for mf in range(m0, m1):
                def ct(kt, which):
                    if grp == "s":
                        base = bC if which == 0 else bS
                        o = base + kt * 512 + (mf - m0) * 128
                    else:
                        o = bdft + which * 512 + kt * 256 + mf * 128
                    return self.WR.v(o, o + 128)
                psGr = self.ps_next(pool=pool7)
                psGi = self.ps_next(pool=pool7)
                for kt in range(KT):
                    self.mm(psGr.v(0, 512), ct(kt, 0), A.vb(HTAPS + kt * 512, HTAPS + (kt + 1) * 512), kt == 0, kt == KT - 1)
                for kt in range(KT):
                    self.mm(psGi.v(0, 512), ct(kt, 1), A.vb(HTAPS + (KT + kt) * 512, HTAPS + (KT + kt + 1) * 512), kt == 0, kt == KT - 1)
                gr = A.vb(HG + mf * 512, HG + (mf + 1) * 512)
                gi = A.vb(HG + (KT + mf) * 512, HG + (KT + mf + 1) * 512)
                self.tt(gr, psGr.v(0, 512), dn, ALU.add)
                self.act(gi, psGi.v(0, 512), AF.Identity)
                for sq in range(nseq):
                    ktb = (4 if grp == "s" else sq * 2)
                    psZr = self.ps_next(pool=pool7)
                    psZi = self.ps_next(pool=pool7)
                    for kt in range(KT):
                        self.mm(psZr.v(0, 512), ct(kt, 0), A.vb(HVZ + (ktb + kt) * 512, HVZ + (ktb + kt + 1) * 512), kt == 0, kt == KT - 1)
                    for kt in range(KT):
                        self.mm(psZi.v(0, 512), ct(kt, 1), A.vb(HVZ + (ktb + kt) * 512, HVZ + (ktb + kt + 1) * 512), kt == 0, kt == KT - 1)
                    ta, tb_, tc, td = (self.tmp_next() for _ in range(4))
                    self.tt(ta, psZr.v(0, 512), gr, ALU.mult)
                    self.tt(tc, psZr.v(0, 512), gi, ALU.mult)
                    self.tt(tb_, psZi.v(0, 512), gi, ALU.mult)
                    self.tt(td, psZi.v(0, 512), gr, ALU.mult)
                    if grp == "s":
                        yr, yi = gr, gi
                    else:
                        yb = HG + 2048 + sq * 2048
                        yr = A.vb(yb + mf * 512, yb + (mf + 1) * 512)
                        yi = A.vb(yb + (2 + mf) * 512, yb + (3 + mf) * 512)
                    self.tt(yr, ta, tb_, ALU.subtract)
                    self.tt(yi, tc, td, ALU.add)
        self.ckpt("fwd_%d_%d_%s" % (order, cb, grp))
        self.S.tag = "inv o%d cb%d %s" % (order, cb, grp)
        xpart = 1 + order
        blocks = [0] if grp == "p" else [1, 2]
        for b in blocks:
            bw = self.wload(self.hy_d["win"][j, xpart * 2 + cb], KC * 512)
            for cc in range(4):
                self.hy_inproj(j, xpart * 8 + cb * 4 + cc, b, bw, cc)
            if grp == "s":
                tbi = b - 1
                bC = self.wload(self.hy_d["dft1024"][2 * 2 + tbi], 4096)
                bS = self.wload(self.hy_d["dft1024"][3 * 2 + tbi], 4096)
            for cc in range(4):
                ps = self.ps_next(pool=pool7)
                if grp == "s":
                    for mf in range(8):
                        self.mm(ps.v(0, 512), A.vb(HG + mf * 512 + cc * 128, HG + mf * 512 + (cc + 1) * 128),
                                self.WR.v(bC + mf * 512, bC + (mf + 1) * 512), mf == 0, False)
                    for mf in range(8):
                        self.mm(ps.v(0, 512), A.vb(HG + (8 + mf) * 512 + cc * 128, HG + (8 + mf) * 512 + (cc + 1) * 128),
                                self.WR.v(bS + mf * 512, bS + (mf + 1) * 512), False, mf == 7)
                else:
                    for sq in range(2):
                        yb = HG + 2048 + sq * 2048
                        for mf in range(2):
                            self.mm(ps.v(sq * 256, (sq + 1) * 256), A.vb(yb + mf * 512 + cc * 128, yb + mf * 512 + (cc + 1) * 128),
                                    self.WR.v(bdft + 1024 + mf * 256, bdft + 1024 + (mf + 1) * 256), mf == 0, False)
                        for mf in range(2):
                            self.mm(ps.v(sq * 256, (sq + 1) * 256), A.vb(yb + (2 + mf) * 512 + cc * 128, yb + (2 + mf) * 512 + (cc + 1) * 128),
                                    self.WR.v(bdft + 1536 + mf * 256, bdft + 1536 + (mf + 1) * 256), False, mf == 1)
                rnc = self.RNC.v(cc, cc + 1)
                if order == 0:
                    t = self.tmp_next()
                    self.stt(t, self.uview(cc, 0, 512), rnc, ps.v(0, 512), ALU.mult, ALU.mult)
                    self.hy_to_tokmajor(t, b, cc)
                else:
                    ch = cb * 4 + cc
                    self.stt(A.vb(HZ2 + ch * NTOK + b * TB, HZ2 + ch * NTOK + (b + 1) * TB), self.uview(cc, 0, 512), rnc,
                             ps.v(0, 512), ALU.mult, ALU.mult)

    def hyena(self, li):
        j = li // 2
        A, S = self.A, self.S
        self.S.tag = "filter_mlp"
        self.hy_filter_mlp(j)
        for cb in range(2):
            self.S.tag = "vbranch cb%d" % cb
            for b in range(NB):
                bw = self.wload(self.hy_d["win"][j, cb], KC * 512)
                for cc in range(4):
                    self.hy_inproj(j, cb * 4 + cc, b, bw, cc)
                for cc in range(4):
                    self.hy_to_tokmajor(self.uview(cc, 0, 512), b, cc)
            self.ckpt("vbranch_%d" % cb)
            for order in range(2):
                for grp in ("p", "s"):
                    self.hy_conv_group(j, order, cb, grp)
                    self.ckpt("conv_%d_%d_%s" % (order, cb, grp))
        self.stat_base = HTAPS
        self.S.tag = "hy oproj"
        for pi in range(2):
            bw = self.wload(self.hy_d["wout"][j, pi], KC * 512)
            for b in range(NB):
                for mi in range(4):
                    m = pi * 4 + mi
                    ps = self.ps_next(pool=(0, 1))
                    for c in range(8):
                        self.mm(ps.v(0, 512), self.WR.v(bw + c * 512 + mi * 128, bw + c * 512 + (mi + 1) * 128),
                                A.vb(HZ2 + c * NTOK + b * TB, HZ2 + c * NTOK + (b + 1) * TB), c == 0, c == 7)
                    n = 0 if b == 0 else 1
                    cidx = (5 * 8 + m) * 2 + n
                    self.resid_and_stats(ps, m, b, self.mod1v(cidx, cidx + 1))
        self.S.tag = "L%d ln1" % li
        self.ln(li, 1, next_k=6)


def _layout_common(inp):
    f = np.float32
    out = {}
    ada_w = np.asarray(inp["ada_w"], f)
    out["adaw"] = np.ascontiguousarray(
        ada_w.reshape(DEPTH, KC, 128, 18, 512).transpose(0, 3, 2, 1, 4).reshape(DEPTH, 18, 128, KC * 512))
    ada_b = np.asarray(inp["ada_b"], f)
    out["adab"] = np.ascontiguousarray(ada_b.reshape(DEPTH, 72, 128).transpose(2, 0, 1).reshape(128, DEPTH * 72))
    for nm, key in (("lng", "ln_g"), ("lnb", "ln_b")):
        a = np.asarray(inp[key], f)
        out[nm] = np.ascontiguousarray(a.reshape(DEPTH * 3, KC, 128).transpose(2, 0, 1).reshape(128, DEPTH * 3 * KC))
    w1 = np.asarray(inp["ffn_w1"], f).reshape(DEPTH * 2, KC, 128, 2, 11, 2, 128)
    out["w1"] = np.ascontiguousarray(w1.transpose(0, 4, 2, 1, 5, 3, 6).reshape(DEPTH * 2, 11, 128, KC * 512))
    w2 = np.asarray(inp["ffn_w2"], f).reshape(DEPTH * 2, NJ, 128, KC, 128)
    out["w2"] = np.ascontiguousarray(w2.transpose(0, 3, 2, 1, 4).reshape(DEPTH * 2, KC, 128, NJ * 128))
    part = np.array([d + 16 if (d % 32) < 16 else d - 16 for d in range(64)])
    qcols, qpcols = [], []
    for cch in range(8):
        for hf in range(2):
            g = (cch // 4) * 2 + hf
            h = g * 4 + (cch % 4)
            qcols += [h * 64 + d for d in range(64)]
            qpcols += [h * 64 + int(part[d]) for d in range(64)]
    kcols = [1024 + g * 64 + d for g in range(4) for d in range(64)]
    kpcols = [1024 + g * 64 + int(part[d]) for g in range(4) for d in range(64)]
    wq = np.asarray(inp["attn_w_qkv"], f)
    pieces = [wq[:, :, qcols[0:512]], wq[:, :, qcols[512:1024]], wq[:, :, qpcols[0:512]], wq[:, :, qpcols[512:1024]],
              wq[:, :, kcols + kpcols], wq[:, :, 1024:1536]]
    wl = np.stack(pieces, axis=1)
    out["wqkv"] = np.ascontiguousarray(wl.reshape(2, 6, KC, 128, 512).transpose(0, 1, 3, 2, 4).reshape(2, 6, 128, KC * 512))
    wo = np.asarray(inp["attn_w_o"], f)[:, qcols, :]
    out["wo"] = np.ascontiguousarray(wo.reshape(2, KC, 128, 2, 512).transpose(0, 3, 2, 1, 4).reshape(2, 2, 128, KC * 512))
    t = np.arange(1024)
    pos = np.stack([t // 64, t % 64], 0).astype(np.float64)
    inv = 10000.0 ** (-np.arange(16, dtype=np.float64) / 16)
    rope = np.zeros((128, 2048), np.float64)
    for p in range(128):
        d = p % 64
        ang = pos[d // 32] * inv[d % 16]
        rope[p, :1024] = np.cos(ang)
        rope[p, 1024:] = np.sin(ang) * (-1.0 if (d % 32) < 16 else 1.0)
    out["rope"] = rope.astype(f)
    kk = np.arange(128)[:, None]
    qq = np.arange(128)[None, :]
    m_lo = np.where(qq <= kk, 0.0, -30000.0)
    m_hi = np.where(kk <= qq, 0.0, -30000.0)
    out["masks"] = np.concatenate([np.tile(m_lo, (1, 4)), np.tile(m_hi, (1, 4))], axis=1).astype(f)
    out["ident"] = np.eye(128, dtype=f)
    sk = np.asarray(inp["attn_sink"], f).reshape(1, 32)
    out["sinkb"] = np.ascontiguousarray(np.broadcast_to(sk, (128, 32)))
    hw = np.asarray(inp["hy_w_in"], f)
    out["hwin"] = np.ascontiguousarray(hw.reshape(2, KC, 128, 6, 512).transpose(0, 3, 2, 1, 4).reshape(2, 6, 128, KC * 512))
    ho = np.asarray(inp["hy_w_out"], f)
    out["hwout"] = np.ascontiguousarray(ho.reshape(2, KC, 128, 2, 512).transpose(0, 3, 2, 1, 4).reshape(2, 2, 128, KC * 512))
    w3 = np.asarray(inp["hy_f_w3"], f).reshape(2, 64, 2, 2, 2, 512)
    out["w3p"] = np.ascontiguousarray(w3.transpose(0, 2, 4, 1, 3, 5).reshape(2, 2, 2, 64, 1024))
    hd = np.asarray(inp["hy_d"], f).reshape(1, 4096)
    out["hyd"] = np.ascontiguousarray(np.broadcast_to(hd, (128, 4096)))
    cw = np.asarray(inp["hy_conv_w"], f)
    cbias = np.asarray(inp["hy_conv_b"], f)
    cp = np.concatenate([cw, cbias[:, None, :]], axis=1)
    out["convp"] = np.ascontiguousarray(cp.reshape(2, 4, 24, 128).transpose(3, 0, 2, 1).reshape(128, 2 * 24 * 4))
    fw1 = np.zeros((128, 128), f)
    fw1[:33, :] = np.asarray(inp["hy_f_w1"], f).transpose(1, 0, 2).reshape(33, 128)
    out["fw1"] = fw1
    fw2 = np.zeros((128, 128), f)
    fw2[:64, :] = np.asarray(inp["hy_f_w2"], f).transpose(1, 0, 2).reshape(64, 128)
    out["fw2"] = fw2
    fb = np.zeros((128, 4), f)
    for jj in range(2):
        fb[:64, jj * 2] = np.asarray(inp["hy_f_b1"], f)[jj]
        fb[:64, jj * 2 + 1] = np.asarray(inp["hy_f_b2"], f)[jj]
    out["fb"] = fb
    out.update(_hyena_constants())
    return out


_HC = {}


def _hyena_constants():
    if _HC:
        return _HC
    f = np.float32
    feats = np.zeros((128, 1280), np.float64)
    col = 0
    for L in (1024, 256):
        t = np.arange(L, dtype=np.float64) / L
        bands = np.arange(1, 17, dtype=np.float64)
        ph = 2 * np.pi * t[:, None] * bands[None]
        ft = np.concatenate([t[:, None], np.sin(ph), np.cos(ph)], -1)
        feats[:33, col:col + L] = ft.T
        col += L
    _HC["feats"] = feats.astype(f)
    max_decay = math.log(1e-2) / 0.3
    min_decay = math.log(1e-2) / 1.5
    deltas = np.abs(np.linspace(min_decay, max_decay, 1024, dtype=np.float32)).astype(np.float64)
    for L in (1024, 256):
        t = (np.arange(L, dtype=np.float32) / L).astype(np.float64)
        dec = np.exp(-t[:, None] * deltas[None])
        _HC["dec%d" % L] = np.ascontiguousarray(dec.reshape(L // 128, 128, 1024)).astype(f)
    L = 1024
    ff = np.arange(L, dtype=np.float64)
    tt = np.arange(L, dtype=np.float64)
    th = np.pi * np.outer(tt, 2 * ff + 1) / (2 * L)
    Ct, nSt = np.cos(th), -np.sin(th)
    Cf, nSf = np.cos(th).T / L, -np.sin(th).T / L
    pcs = []
    for M in (Ct, nSt, Cf, nSf):
        for blk in range(2):
            pcs.append(M[:, blk * 512:(blk + 1) * 512].reshape(8, 128, 512).transpose(1, 0, 2).reshape(128, 4096))
    _HC["dft1024"] = np.ascontiguousarray(np.stack(pcs, 0)).astype(f)
    L = 256
    ff = np.arange(L, dtype=np.float64)
    tt = np.arange(L, dtype=np.float64)
    th = np.pi * np.outer(tt, 2 * ff + 1) / (2 * L)
    mats = [np.cos(th), -np.sin(th), np.cos(th).T / L, -np.sin(th).T / L]
    _HC["dft256"] = np.ascontiguousarray(np.concatenate(
        [M.reshape(2, 128, 256).transpose(1, 0, 2).reshape(128, 512) for M in mats], axis=1)).astype(f)
    return _HC


def _per_core(inp, c):
    f = np.float32
    xp = np.asarray(inp["x_prompt"], f)[2 * c:2 * c + 2].reshape(512, D)
    xs = np.asarray(inp["x_sample"], f)[c]
    x = np.concatenate([xp, xs], axis=0)
    m = {"xT": np.ascontiguousarray(x.T)}
    cond = np.stack([np.asarray(inp["c_ctx"], f), np.asarray(inp["c"], f)[c]], axis=0)
    m["condT"] = np.ascontiguousarray(cond.reshape(2, KC, 128).transpose(2, 1, 0).reshape(128, KC * 2))
    m["ck"] = np.ascontiguousarray(np.asarray(inp["cache_k"], f)[c].reshape(2, 256, 256))
    m["cv"] = np.ascontiguousarray(np.asarray(inp["cache_v"], f)[c].reshape(2, 256, 256))
    return m


def run(inp, cfg):
    common = _layout_common(inp)
    b = Builder(cfg)
    nc = b.build()
    in_maps = []
    for c in range(NCORES):
        m = dict(common)
        m.update(_per_core(inp, c))
        in_maps.append(m)
    res = run_bass_kernel_spmd(nc, in_maps, core_ids=list(range(NCORES)))
    b.stack.close()
    return res.results


def kernel(**inputs):
    cfg = {"mixers": "all"}
    results = run(inputs, cfg)
    return assemble(results)


def assemble(results):
    yp = np.zeros((16, 256, D), np.float32)
    ys = np.zeros((8, 1024, D), np.float32)
    for c in range(NCORES):
        y = np.ascontiguousarray(results[c]["yT"].T)
        yp[2 * c:2 * c + 2] = y[:512].reshape(2, 256, D)
        ys[c] = y[512:]
    nk = np.zeros((16, 2, 256, 4, 64), np.float32)
    nv = np.zeros((16, 2, 256, 4, 64), np.float32)
    for c in range(NCORES):
        k = results[c]["nk"].reshape(2, 2, 256, 4, 64)
        v = results[c]["nv"].reshape(2, 2, 256, 4, 64)
        nk[2 * c:2 * c + 2] = k.transpose(1, 0, 2, 3, 4)
        nv[2 * c:2 * c + 2] = v.transpose(1, 0, 2, 3, 4)
    return (yp, ys, nk, nv)
```

```python
import math
from collections import defaultdict
from contextlib import ExitStack

import numpy as np
import concourse.bass as bass
import concourse.mybir as mybir
from concourse.bass_utils import run_bass_kernel_spmd

F32 = mybir.dt.float32
BF16 = mybir.dt.bfloat16
AF = mybir.ActivationFunctionType
ALU = mybir.AluOpType

D = 1024
KC = 8
DFF = 2816
NJ = 22
DEPTH = 4
NTOK = 1536
NB = 3
TB = 512
ALPHA = (2 * DEPTH) ** 0.25
LN_EPS = 1e-5
NCORES = 8
SLOT_COLS = 4096
NSLOTS = 3


ACOLS = 18944
HZ2 = 0
HVZ = 12288
HTAPS = 18432
HG = 26624
HA2 = 34816
HW3 = 36096

AQ = 0
AK = 8 * NTOK
AKC = AK + 2 * NTOK
AVP = AKC + 512
APT = AVP + 14 * 512
NPT = 5
ART = (APT + NPT * 512) // 2
AROPE = ART + 1024
ASTG = AROPE + 2048


class View:
    __slots__ = ("ap", "keys")

    def __init__(self, ap, keys):
        self.ap = ap
        self.keys = keys


class Tile:
    def __init__(self, name, handle, ncols, gran):
        self.name = name
        self.h = handle
        self.ncols = ncols
        self.gran = gran

    def _keys(self, b0, b1, p0, p1):
        halves = (0, 1)
        if not self.name.startswith("ps"):
            if p0 >= 64:
                halves = (1,)
            elif p1 <= 64:
                halves = (0,)
        return [(self.name, k, h) for k in range(b0 // self.gran, (b1 - 1) // self.gran + 1) for h in halves]

    def v(self, c0, c1, p0=0, p1=128):
        assert 0 <= c0 < c1 <= self.ncols, (self.name, c0, c1, self.ncols)
        return View(self.h[p0:p1, c0:c1], self._keys(c0, c1, p0, p1))

    def vb(self, c0, c1, p0=0, p1=128):
        assert c0 % 2 == 0 and c1 % 2 == 0
        b0, b1 = c0 // 2, c1 // 2
        assert 0 <= b0 < b1 <= self.ncols, (self.name, c0, c1, self.ncols)
        return View(self.h[p0:p1, b0:b1].bitcast(BF16), self._keys(b0, b1, p0, p1))


class Op:
    __slots__ = ("eng", "fn", "reads", "writes", "kind", "slot", "slot_count", "pos", "signal",
                 "sigcount", "waits", "tag")


class Sched:
    def __init__(self, nc):
        self.nc = nc
        self.ops = []
        self.slot_counts = defaultdict(int)
        self.slot_last = {}
        self.tag = ""
        self.names = {}

    def op(self, eng, fn, reads=(), writes=()):
        o = Op()
        o.eng, o.fn, o.kind, o.slot, o.slot_count = eng, fn, "c", None, 0
        o.tag = self.tag
        o.reads = [k for v in reads for k in v.keys]
        o.writes = [k for v in writes for k in v.keys]
        o.writes += [k for k in o.reads if k[0].startswith("ps")]
        o.signal = False
        self.ops.append(o)
        return o

    def dma(self, queue, out, in_, slot, reads=(), writes=()):
        o = Op()
        o.eng, o.kind, o.slot = queue, "d", slot
        o.tag = self.tag
        self.slot_counts[slot] += 1
        o.slot_count = self.slot_counts[slot]
        oap = out.ap if isinstance(out, View) else out
        iap = in_.ap if isinstance(in_, View) else in_
        o.fn = lambda e, oap=oap, iap=iap: e.dma_start(out=oap, in_=iap)
        o.reads = [k for v in reads for k in v.keys]
        o.writes = [k for v in writes for k in v.keys]
        if isinstance(out, View):
            o.writes += out.keys
        if isinstance(in_, View):
            o.reads += in_.keys
        o.signal = False
        self.ops.append(o)
        return o

    def finalize(self):
        ops = self.ops
        last_w = {}
        readers = {}
        cnt = defaultdict(int)
        waited = defaultdict(lambda: -1)
        slot_prev = {}
        for i, op in enumerate(ops):
            op.pos = cnt[op.eng]
            cnt[op.eng] += 1
            deps = set()
            for k in op.reads:
                j = last_w.get(k)
                if j is not None:
                    deps.add(j)
            for k in op.writes:
                j = last_w.get(k)
                if j is not None:
                    deps.add(j)
                deps.update(readers.get(k, ()))
            if op.kind == "d" and op.slot in slot_prev:
                deps.add(slot_prev[op.slot])
            op.waits = []
            for j in sorted(deps):
                src = ops[j]
                if src.kind == "d":
                    key = (op.eng, "slot", src.slot)
                    if waited[key] >= src.slot_count:
                        continue
                    waited[key] = src.slot_count
                    op.waits.append(j)
                else:
                    if src.eng == op.eng and op.kind == "c":
                        if op.eng == "pe":
                            continue
                    key = (op.eng, src.eng)
                    if waited[key] >= src.pos:
                        continue
                    waited[key] = src.pos
                    op.waits.append(j)
                    src.signal = True
            for k in op.reads:
                readers.setdefault(k, []).append(i)
            for k in op.writes:
                last_w[k] = i
                readers[k] = []
            if op.kind == "d":
                slot_prev[op.slot] = i
        sc = defaultdict(int)
        for op in ops:
            if op.kind == "c" and op.signal:
                sc[op.eng] += 1
            op.sigcount = sc[op.eng]

    def emit(self, stack):
        nc = self.nc
        self.finalize()
        ops = self.ops
        engs = ["pe", "act", "dve", "pool", "sp"]
        sems = {e: stack.enter_context(nc.semaphore("s_" + e)) for e in engs}
        slot_sems = {s: stack.enter_context(nc.semaphore("d_" + str(s))) for s in self.slot_counts}
        streams = {e: [o for o in ops if o.eng == e] for e in engs}
        block = stack.enter_context(nc.Block())

        def run(e, ename):
            for o in streams[ename]:
                for j in o.waits:
                    src = ops[j]
                    if src.kind == "d":
                        e.wait_ge(slot_sems[src.slot], 16 * src.slot_count)
                    else:
                        e.wait_ge(sems[src.eng], src.sigcount)
                ins = o.fn(e)
                try:
                    self.names[ins.ins.name] = o.tag
                except Exception:
                    pass
                if o.kind == "d":
                    ins.then_inc(slot_sems[o.slot], 16)
                elif o.signal:
                    ins.then_inc(sems[ename], 1)
            if ename == "sp":
                for s, n in self.slot_counts.items():
                    e.wait_ge(slot_sems[s], 16 * n)

        @block.tensor
        def _(e):
            run(e, "pe")

        @block.scalar
        def _(e):
            run(e, "act")

        @block.vector
        def _(e):
            run(e, "dve")

        @block.gpsimd
        def _(e):
            run(e, "pool")

        @block.sync
        def _(e):
            run(e, "sp")


class StopBuild(Exception):
    pass


class Builder:
    def ckpt(self, name):
        if self.cfg.get("stop_at") == name:
            raise StopBuild()

    def __init__(self, cfg):
        self.cfg = cfg
        self.nc = bass.Bass("TRN2", target_bir_lowering=False)
        self.stack = ExitStack()
        self.S = Sched(self.nc)
        self.wcount = 0
        self.tmp_i = 0
        self.ps_i = 0

    def dram_in(self, name, shape, dt=F32):
        return self.nc.dram_tensor(name, list(shape), dt, kind="ExternalInput").ap()

    def dram_out(self, name, shape, dt=F32):
        return self.nc.dram_tensor(name, list(shape), dt, kind="ExternalOutput").ap()

    def sb(self, name, ncols, dt, gran=512):
        h = self.stack.enter_context(self.nc.sbuf_tensor(name, [128, ncols], dt))
        return Tile(name, h, ncols, gran)

    def psum_banks(self):
        self.PS = []
        for i in range(8):
            h = self.stack.enter_context(self.nc.psum_tensor("ps%d" % i, [128, 512], F32))
            self.PS.append(Tile("ps%d" % i, h, 512, 512))

    def ps_next(self, pool=(0, 1, 2, 3)):
        b = pool[self.ps_i % len(pool)]
        self.ps_i += 1
        return self.PS[b]

    def tmp_next(self):
        i = self.tmp_i % (self.NTMP - 2)
        self.tmp_i += 1
        return self.TMP.v(i * 512, (i + 1) * 512)

    def wload(self, dram_ap, ncols):
        s = self.wcount % NSLOTS
        self.wcount += 1
        v = self.WR.v(s * SLOT_COLS, s * SLOT_COLS + ncols)
        self.S.dma("pool", v, dram_ap, slot="w%d" % s)
        return s * SLOT_COLS

    def mm(self, ps_view, lhsT, rhs, start, stop):
        self.S.op("pe", lambda e, o=ps_view.ap, l=lhsT.ap, r=rhs.ap, st=start, sp=stop:
                  e.matmul(o, l, r, start=st, stop=sp), reads=[lhsT, rhs], writes=[ps_view])

    def act(self, out, in_, func, bias=None, scale=None, extra_reads=()):
        kw = {}
        if bias is not None:
            kw["bias"] = bias.ap if isinstance(bias, View) else bias
        if scale is not None:
            kw["scale"] = scale.ap if isinstance(scale, View) else scale
        rd = [in_] + [x for x in (bias, scale) if isinstance(x, View)] + list(extra_reads)
        self.S.op("act", lambda e, o=out.ap, i=in_.ap, f=func, kw=kw: e.activation(o, i, f, **kw),
                  reads=rd, writes=[out])

    def tt(self, out, in0, in1, op, eng="dve"):
        self.S.op(eng, lambda e, o=out.ap, a=in0.ap, b=in1.ap, op=op: e.tensor_tensor(o, a, b, op),
                  reads=[in0, in1], writes=[out])

    def ts(self, out, in0, s1, s2, op0, op1=None, eng="dve"):
        rd = [in0] + [x for x in (s1, s2) if isinstance(x, View)]
        a1 = s1.ap if isinstance(s1, View) else s1
        a2 = s2.ap if isinstance(s2, View) else s2
        if op1 is None:
            self.S.op(eng, lambda e, o=out.ap, a=in0.ap: e.tensor_scalar(o, a, a1, None, op0),
                      reads=rd, writes=[out])
        else:
            self.S.op(eng, lambda e, o=out.ap, a=in0.ap: e.tensor_scalar(o, a, a1, a2, op0, op1),
                      reads=rd, writes=[out])

    def recip(self, v):
        self.act(v, v, AF.Ln)
        self.act(v, v, AF.Exp, scale=-1.0)

    def stt(self, out, in0, scalar, in1, op0, op1, eng="dve"):
        rd = [in0, in1] + ([scalar] if isinstance(scalar, View) else [])
        sc = scalar.ap if isinstance(scalar, View) else scalar
        self.S.op(eng, lambda e, o=out.ap, a=in0.ap, b=in1.ap: e.scalar_tensor_tensor(o, a, sc, b, op0, op1),
                  reads=rd, writes=[out])

    def build(self):
        cfg = self.cfg
        nlayers = cfg.get("nlayers", DEPTH)
        xT_d = self.dram_in("xT", [D, NTOK])
        cond_d = self.dram_in("condT", [128, KC * 2])
        adaw_d = self.dram_in("adaw", [DEPTH, 18, 128, KC * 512])
        adab_d = self.dram_in("adab", [128, DEPTH * 72])
        lng_d = self.dram_in("lng", [128, DEPTH * 3 * KC])
        lnb_d = self.dram_in("lnb", [128, DEPTH * 3 * KC])
        w1_d = self.dram_in("w1", [DEPTH * 2, 11, 128, KC * 512])
        w2_d = self.dram_in("w2", [DEPTH * 2, KC, 128, NJ * 128])
        yT_d = self.dram_out("yT", [D, NTOK])
        wqkv_d = self.dram_in("wqkv", [2, 6, 128, KC * 512])
        wo_d = self.dram_in("wo", [2, 2, 128, KC * 512])
        rope_d = self.dram_in("rope", [128, 2048])
        masks_d = self.dram_in("masks", [128, 1024])
        ident_d = self.dram_in("ident", [128, 128])
        sinkb_d = self.dram_in("sinkb", [128, 32])
        ck_d = self.dram_in("ck", [2, 256, 256])
        cv_d = self.dram_in("cv", [2, 256, 256])
        nk_d = self.dram_out("nk", [2, 512, 256])
        nv_d = self.dram_out("nv", [2, 512, 256])
        self.attn_d = (wqkv_d, wo_d, rope_d, ck_d, cv_d, nk_d, nv_d)
        self.hy_d = dict(
            win=self.dram_in("hwin", [2, 6, 128, KC * 512]),
            wout=self.dram_in("hwout", [2, 2, 128, KC * 512]),
            dft1024=self.dram_in("dft1024", [8, 128, 4096]),
            dft256=self.dram_in("dft256", [128, 2048]),
            dec1024=self.dram_in("dec1024", [8, 128, 1024]),
            dec256=self.dram_in("dec256", [2, 128, 1024]),
            feats=self.dram_in("feats", [128, 1280]),
            w3p=self.dram_in("w3p", [2, 2, 2, 64, 1024]),
            hyd=self.dram_in("hyd", [128, 2 * 2 * 1024]),
            convp=self.dram_in("convp", [128, 2 * 24 * 4]),
            fw1=self.dram_in("fw1", [128, 128]),
            fw2=self.dram_in("fw2", [128, 128]),
            fb=self.dram_in("fb", [128, 4]),
        )

        self.X = self.sb("X", KC * NTOK, F32)
        self.H = self.sb("H", KC * NTOK, BF16)
        self.A = self.sb("A", ACOLS, F32, gran=64)
        self.WR = self.sb("WR", NSLOTS * SLOT_COLS, BF16, gran=SLOT_COLS)
        self.NTMP = 6
        self.TMP = self.sb("TMP", self.NTMP * 512, F32)
        self.LNA = self.sb("LNA", NB * 512, F32)
        self.LNB = self.sb("LNB", NB * 512, F32)
        self.ONES = self.sb("ONES", 128, F32, gran=128)
        self.ONESB = self.sb("ONESB", 128, BF16, gran=128)
        self.stat_i = 0
        self.stat_base = 33792
        self.EPS = self.sb("EPS", 1, F32, gran=1)
        self.COND = self.sb("COND", KC * 2, F32, gran=16)
        self.SC = self.sb("SC", KC * 2, BF16, gran=16)
        self.MOD = self.sb("MOD", 2 * 72 * 2, F32, gran=2)
        self.MOD1 = self.sb("MOD1", 2 * 72 * 2, F32, gran=2)
        self.mb = 0
        self.bgq = []
        self.ADAB = self.sb("ADAB", DEPTH * 72, F32, gran=72)
        self.LNG = self.sb("LNG", DEPTH * 3 * KC, F32, gran=KC)
        self.LNBT = self.sb("LNBT", DEPTH * 3 * KC, F32, gran=KC)
        self.FG = self.sb("FG", KC * 2, F32, gran=16)
        self.FB = self.sb("FB", KC * 2, F32, gran=16)
        self.IDENT = self.sb("IDENT", 128, F32, gran=128)
        self.ONESK = self.sb("ONESK", 128, BF16, gran=128)
        self.MASKS = self.sb("MASKS", 1024, BF16, gran=512)
        self.IDB = self.sb("IDB", 128, BF16, gran=128)
        self.ESINK = self.sb("ESINK", 16, F32, gran=16)
        self.SINKB = self.sb("SINKB", 32, F32, gran=32)
        self.CONVP = self.sb("CONVP", 2 * 24 * 4, F32, gran=4)
        self.FW1 = self.sb("FW1", 128, F32, gran=128)
        self.FW2 = self.sb("FW2", 128, F32, gran=128)
        self.FBT = self.sb("FBT", 4, F32, gran=4)
        self.FBP = self.sb("FBP", 4, F32, gran=4)
        self.NEGPI = self.sb("NEGPI", 1, F32, gran=1)
        self.ABSB = self.sb("ABSB", 1024, BF16, gran=512)
        self.RNC = self.sb("RNC", 4, F32, gran=4)
        self.abs_i = 0
        self.pt_i = 0
        self.rt_i = 0
        self.stg_i = 0
        self.psum_banks()
        S = self.S
        S.dma("sp", self.IDENT.v(0, 128), ident_d, slot="c4")
        S.dma("sp", self.SINKB.v(0, 32), sinkb_d, slot="c5")
        S.dma("pool", self.MASKS.v(0, 1024), masks_d, slot="c6")
        S.dma("pool", self.IDB.v(0, 128), ident_d, slot="c11")
        S.op("dve", lambda e: e.memset(self.ONESK.h[:, :], 1.0), writes=[self.ONESK.v(0, 128)])
        S.op("dve", lambda e: e.memset(self.NEGPI.h[:, :], -math.pi), writes=[self.NEGPI.v(0, 1)])
        S.dma("sp", self.CONVP.v(0, 192), self.hy_d["convp"], slot="c7")
        S.dma("sp", self.FW1.v(0, 128), self.hy_d["fw1"], slot="c8")
        S.dma("sp", self.FW2.v(0, 128), self.hy_d["fw2"], slot="c9")
        S.dma("sp", self.FBT.v(0, 4), self.hy_d["fb"], slot="c10")
        self.ts(self.FBP.v(0, 4), self.FBT.v(0, 4), 17.0 * math.pi, None, ALU.add)

        for kc in range(KC):
            S.dma("sp", self.X.v(kc * NTOK, (kc + 1) * NTOK), xT_d[kc * 128:(kc + 1) * 128, :], slot="x%d" % kc)
        S.dma("sp", self.COND.v(0, KC * 2), cond_d, slot="c0")
        S.dma("sp", self.ADAB.v(0, DEPTH * 72), adab_d, slot="c1")
        S.dma("sp", self.LNG.v(0, DEPTH * 3 * KC), lng_d, slot="c2")
        S.dma("sp", self.LNBT.v(0, DEPTH * 3 * KC), lnb_d, slot="c3")
        S.op("dve", lambda e: e.memset(self.ONES.h[:, :], 1.0 / D), writes=[self.ONES.v(0, 128)])
        S.op("dve", lambda e: e.memset(self.ONESB.h[:, :], 1.0 / D), writes=[self.ONESB.v(0, 128)])
        S.op("dve", lambda e: e.memset(self.EPS.h[:, :], LN_EPS), writes=[self.EPS.v(0, 1)])
        self.act(self.SC.v(0, KC * 2), self.COND.v(0, KC * 2), AF.Silu)

        try:
            self.layers(nlayers, adaw_d, w1_d, w2_d)
        except StopBuild:
            pass

        for kc in range(KC):
            S.dma("sp", yT_d[kc * 128:(kc + 1) * 128, :], self.X.v(kc * NTOK, (kc + 1) * NTOK), slot="y%d" % kc)
        S.emit(self.stack)
        return self.nc

    def layers(self, nlayers, adaw_d, w1_d, w2_d):
        cfg = self.cfg
        self.adaw_d = adaw_d
        for li in range(nlayers):
            self.S.tag = "L%d ada" % li
            if li == 0:
                for pc in range(6):
                    self.ada_piece(0, pc)
                self.ada_finish(0, ks=(1, 2))
                self.ada_enqueue(0, first=6)
            else:
                self.bg(100)
            self.mb = (li % 2) * 144
            if li == 0:
                self.mod_in(li, 0)
            self.S.tag = "L%d ffn0" % li
            self.ffn(li, 0, w1_d, w2_d)
            self.S.tag = "L%d ln0" % li
            if li == 0:
                self.bg(100)
            self.ln(li, 0, next_k=3)
            self.S.tag = "L%d mixer" % li
            mx = cfg.get("mixers", "all")
            if li % 2 == 0 and mx in ("all", "attn"):
                self.attention(li)
            elif li % 2 == 1 and mx in ("all", "hyena"):
                self.hyena(li)
            else:
                self.ffn_dummy_mixer(li)
            self.S.tag = "L%d ffn1" % li
            if li + 1 < nlayers:
                self.ada_enqueue(li + 1)
            self.ffn(li, 1, w1_d, w2_d)
            self.S.tag = "L%d ln2" % li
            if li + 1 < nlayers:
                self.bg(100)
                self.ln(li, 2, next_k=0, mb=((li + 1) % 2) * 144)
            else:
                self.ln(li, 2, next_k=None)

    def ada_piece(self, li, pc, base=None):
        mb = (li % 2) * 144
        adaw_d = self.adaw_d
        if base is None:
            base = self.wload(adaw_d[li, pc], KC * 512)
        for mi in range(4):
            m = pc * 4 + mi
            ps = self.ps_next(pool=(0, 1))
            for kc in range(KC):
                self.mm(ps.v(0, 2), self.WR.v(base + kc * 512 + mi * 128, base + kc * 512 + (mi + 1) * 128),
                        self.SC.v(kc * 2, kc * 2 + 2), kc == 0, kc == KC - 1)
            self.ts(self.MOD.v(mb + m * 2, mb + m * 2 + 2), ps.v(0, 2),
                    self.ADAB.v(li * 72 + m, li * 72 + m + 1), None, ALU.add)

    def ada_finish(self, li, ks=(1, 2, 4, 5, 7, 8)):
        mb = (li % 2) * 144
        for k in ks:
            src = self.MOD.v(mb + k * 16, mb + k * 16 + 16)
            dst = self.MOD1.v(mb + k * 16, mb + k * 16 + 16)
            if k in (1, 4, 7):
                self.ts(dst, src, 1.0, None, ALU.add)
            elif k in (2, 8):
                self.ts(dst, src, 0.5, None, ALU.mult)
            else:
                self.ts(dst, src, 1.0, None, ALU.mult)

    def ada_enqueue(self, li, first=0):
        for pc in range(first, 18):
            self.bgq.append(lambda li=li, pc=pc: self.ada_piece(li, pc))
        if first == 0:
            self.bgq.append(lambda li=li: self.ada_finish(li))
        else:
            self.bgq.append(lambda li=li: self.ada_finish(li, ks=(4, 5, 7, 8)))

    def bg(self, n=1):
        for _ in range(n):
            if not self.bgq:
                return
            tag = self.S.tag
            self.S.tag = "bg ada"
            self.bgq.pop(0)()
            self.S.tag = tag

    def modv(self, c0, c1):
        return self.MOD.v(self.mb + c0, self.mb + c1)

    def mod1v(self, c0, c1):
        return self.MOD1.v(self.mb + c0, self.mb + c1)

    def tokgroups(self):
        return [(0, 0, 512), (1, 512, 1024), (1, 1024, 1536)]

    def mod_in(self, li, k):
        for kc in range(KC):
            for (n, t0, t1) in self.tokgroups():
                c = (k * 8 + kc) * 2 + n
                c1 = ((k + 1) * 8 + kc) * 2 + n
                self.act(self.H.v(kc * NTOK + t0, kc * NTOK + t1), self.X.v(kc * NTOK + t0, kc * NTOK + t1),
                         AF.Identity, bias=self.modv(c, c + 1), scale=self.mod1v(c1, c1 + 1))

    def ffn(self, li, s, w1_d, w2_d):
        self.stat_base = 33792
        kg = 2 if s == 0 else 8
        fi = li * 2 + s
        def unit(base, pc, b):
            for pr in range(2):
                j = pc * 2 + pr
                psg = self.ps_next(pool=(0, 1, 2, 3, 4, 5, 6, 7))
                psu = self.ps_next(pool=(0, 1, 2, 3, 4, 5, 6, 7))
                for half, ps in ((0, psg), (1, psu)):
                    off = pr * 256 + half * 128
                    for kc in range(KC):
                        self.mm(ps.v(0, 512), self.WR.v(base + kc * 512 + off, base + kc * 512 + off + 128),
                                self.H.v(kc * NTOK + b * TB, kc * NTOK + (b + 1) * TB), kc == 0, kc == KC - 1)
                t = self.tmp_next()
                self.act(t, psg.v(0, 512), AF.Silu)
                self.tt(self.A.vb(j * NTOK + b * TB, j * NTOK + (b + 1) * TB), t, psu.v(0, 512), ALU.mult)

        GP = NSLOTS
        bases = [self.wload(w1_d[fi, pc], KC * 512) for pc in range(GP)]
        for b in range(NB):
            for pc in range(GP):
                unit(bases[pc], pc, b)
        self.bg(GP)
        for pc in range(GP, 11):
            base = self.wload(w1_d[fi, pc], KC * 512)
            for b in range(NB):
                unit(base, pc, b)
            self.bg()
        self.S.tag = self.S.tag + " p2"
        for m in range(KC):
            base = self.wload(w2_d[fi, m], NJ * 128)
            for b in range(NB):
                ps = self.ps_next(pool=(0, 1))
                for j in range(NJ):
                    self.mm(ps.v(0, 512), self.WR.v(base + j * 128, base + (j + 1) * 128),
                            self.A.vb(j * NTOK + b * TB, j * NTOK + (b + 1) * TB), j == 0, j == NJ - 1)
                n = 0 if b == 0 else 1
                c = (kg * 8 + m) * 2 + n
                self.resid_and_stats(ps, m, b, self.mod1v(c, c + 1))
            self.bg()

    def resid_and_stats(self, ps, m, b, gate):
        t = self.tmp_next()
        self.act(t, ps.v(0, 512), AF.Identity, scale=gate)
        xv = self.X.v(m * NTOK + b * TB, m * NTOK + (b + 1) * TB)
        self.stt(xv, xv, ALPHA, t, ALU.mult, ALU.add)
        t2 = self.tmp_next()
        self.tt(t2, xv, xv, ALU.mult)
        sb = self.stat_base + (self.stat_i % 2) * 2048
        self.stat_i += 1
        yh, yl, sh, sl = (self.A.vb(sb + i * 512, sb + (i + 1) * 512) for i in range(4))
        self.act(yh, xv, AF.Identity)
        self.tt(yl, xv, yh, ALU.subtract)
        self.act(sh, t2, AF.Identity)
        self.tt(sl, t2, sh, ALU.subtract)
        self.flush_stats()
        self.pending_stats = (yh, yl, sh, sl, m, b)

    def flush_stats(self):
        p = getattr(self, "pending_stats", None)
        if p is None:
            return
        yh, yl, sh, sl, m, b = p
        self.mm(self.PS[2 + b].v(0, 512), self.ONESB.v(0, 128), yh, m == 0, False)
        self.mm(self.PS[2 + b].v(0, 512), self.ONESB.v(0, 128), yl, False, m == KC - 1)
        self.mm(self.PS[5 + b].v(0, 512), self.ONESB.v(0, 128), sh, m == 0, False)
        self.mm(self.PS[5 + b].v(0, 512), self.ONESB.v(0, 128), sl, False, m == KC - 1)
        self.pending_stats = None

    def ffn_dummy_mixer(self, li):
        for m in range(KC):
            for b in range(NB):
                xv = self.X.v(m * NTOK + b * TB, m * NTOK + (b + 1) * TB)
                self.ts(xv, xv, ALPHA, None, ALU.mult)
                t2 = self.tmp_next()
                self.tt(t2, xv, xv, ALU.mult)
                self.mm(self.PS[2 + b].v(0, 512), self.ONES.v(0, 128), xv, m == 0, m == KC - 1)
                self.mm(self.PS[5 + b].v(0, 512), self.ONES.v(0, 128), t2, m == 0, m == KC - 1)
        self.ln(li, 1, next_k=6)

    def ln(self, li, s, next_k, mb=None):
        self.flush_stats()
        if mb is None:
            mb = self.mb
        gi = (li * 3 + s) * KC
        if next_k is not None:
            for kc in range(KC):
                cs = mb + ((next_k + 1) * 8 + kc) * 2
                ch = mb + (next_k * 8 + kc) * 2
                self.ts(self.FG.v(kc * 2, kc * 2 + 2), self.MOD1.v(cs, cs + 2),
                        self.LNG.v(gi + kc, gi + kc + 1), None, ALU.mult)
                self.stt(self.FB.v(kc * 2, kc * 2 + 2), self.MOD1.v(cs, cs + 2),
                         self.LNBT.v(gi + kc, gi + kc + 1), self.MOD.v(ch, ch + 2), ALU.mult, ALU.add)
        for b in range(NB):
            mean = self.PS[2 + b].v(0, 512)
            e2 = self.PS[5 + b].v(0, 512)
            la = self.LNA.v(b * 512, (b + 1) * 512)
            lb = self.LNB.v(b * 512, (b + 1) * 512)
            t = self.tmp_next()
            self.act(t, mean, AF.Square)
            self.tt(la, e2, t, ALU.subtract)
            self.act(la, la, AF.Ln, bias=self.EPS.v(0, 1))
            self.act(la, la, AF.Exp, scale=-0.5)
            self.stt(lb, mean, -1.0, la, ALU.mult, ALU.mult)
        for (n, t0, t1) in ((0, 0, 512), (1, 512, 1536)):
            la = self.LNA.v(t0, t1)
            lb = self.LNB.v(t0, t1)
            for kc in range(KC):
                xv = self.X.v(kc * NTOK + t0, kc * NTOK + t1)
                self.tt(xv, xv, la, ALU.mult)
                self.tt(xv, xv, lb, ALU.add)
                if next_k is not None:
                    self.act(self.H.v(kc * NTOK + t0, kc * NTOK + t1), xv, AF.Identity,
                             bias=self.FB.v(kc * 2 + n, kc * 2 + n + 1),
                             scale=self.FG.v(kc * 2 + n, kc * 2 + n + 1))
                self.act(xv, xv, AF.Identity, bias=self.LNBT.v(gi + kc, gi + kc + 1),
                         scale=self.LNG.v(gi + kc, gi + kc + 1))

    def proj_rope(self, bw, offw, bp, offp, b, dest):
        A = self.A
        ps = self.ps_next(pool=(0, 1, 2, 3, 4, 5, 6, 7))
        for kc in range(KC):
            self.mm(ps.v(0, 512), self.WR.v(bw + kc * 512 + offw, bw + kc * 512 + offw + 128),
                    self.H.v(kc * NTOK + b * TB, kc * NTOK + (b + 1) * TB), kc == 0, kc == KC - 1)
        if b == 0:
            self.act(dest, ps.v(0, 512), AF.Identity)
            return
        ps2 = self.ps_next(pool=(0, 1, 2, 3, 4, 5, 6, 7))
        for kc in range(KC):
            self.mm(ps2.v(0, 512), self.WR.v(bp + kc * 512 + offp, bp + kc * 512 + offp + 128),
                    self.H.v(kc * NTOK + b * TB, kc * NTOK + (b + 1) * TB), kc == 0, kc == KC - 1)
        t0 = (b - 1) * 512
        cosv = A.v(AROPE + t0, AROPE + t0 + 512)
        sinv = A.v(AROPE + 1024 + t0, AROPE + 1024 + t0 + 512)
        t1 = self.tmp_next()
        t2 = self.tmp_next()
        self.tt(t1, ps.v(0, 512), cosv, ALU.mult)
        self.tt(t2, ps2.v(0, 512), sinv, ALU.mult)
        self.tt(dest, t1, t2, ALU.add)

    def copy_any(self, out, in_):
        self.cp_i = getattr(self, "cp_i", 0) + 1
        if self.cp_i % 2:
            self.act(out, in_, AF.Identity)
        else:
            self.S.op("dve", lambda e, o=out.ap, i=in_.ap: e.tensor_copy(o, i), reads=[in_], writes=[out])

    def attention(self, li):
        wqkv_d, wo_d, rope_d, ck_d, cv_d, nk_d, nv_d = self.attn_d
        j = li // 2
        A = self.A
        S = self.S
        S.dma("sp", A.v(AROPE, AROPE + 2048), rope_d, slot="rope")
        self.act(self.ESINK.v(0, 16), self.SINKB.v(j * 16, j * 16 + 16), AF.Exp)
        vpall = A.vb(AVP, AVP + 14 * 512)
        S.op("dve", lambda e, o=vpall.ap: e.memset(o, 1.0), writes=[vpall])
        stgk = ASTG + 0
        stgv = ASTG + 512
        for tl in range(2):
            S.dma("sp", A.v(stgk + tl * 256, stgk + (tl + 1) * 256), ck_d[j, tl * 128:(tl + 1) * 128, :], slot="ck%d" % tl)
            S.dma("sp", A.v(stgv + tl * 256, stgv + (tl + 1) * 256), cv_d[j, tl * 128:(tl + 1) * 128, :], slot="cv%d" % tl)
        for tl in range(2):
            for g in range(4):
                self.copy_any(A.vb(AVP + (12 + tl) * 512 + g * 128 + (g % 2) * 64, AVP + (12 + tl) * 512 + g * 128 + (g % 2) * 64 + 64),
                              A.v(stgv + tl * 256 + g * 64, stgv + tl * 256 + g * 64 + 64))
            for c in range(2):
                ps = self.ps_next()
                src = A.v(stgk + tl * 256 + c * 128, stgk + tl * 256 + (c + 1) * 128)
                self.S.op("pe", lambda e, o=ps.v(0, 128).ap, i=src.ap, idn=self.IDENT.h[:, :]: e.transpose(o, i, idn),
                          reads=[src, self.IDENT.v(0, 128)], writes=[ps.v(0, 128)])
                self.copy_any(A.vb(AKC + c * 256 + tl * 128, AKC + c * 256 + (tl + 1) * 128), ps.v(0, 128))
        stage = self.cfg.get("attn_stage", 9)
        if stage <= 1:
            return self.ffn_dummy_mixer(li)
        for half in range(2):
            bq = self.wload(wqkv_d[j, half], KC * 512)
            bp = self.wload(wqkv_d[j, 2 + half], KC * 512)
            for b in range(NB):
                for ci in range(4):
                    c = half * 4 + ci
                    self.proj_rope(bq, ci * 128, bp, ci * 128, b,
                                   A.vb(AQ + c * NTOK + b * TB, AQ + c * NTOK + (b + 1) * TB))
        bk = self.wload(wqkv_d[j, 4], KC * 512)
        for b in range(NB):
            for c in range(2):
                self.proj_rope(bk, c * 128, bk, 256 + c * 128, b,
                               A.vb(AK + c * NTOK + b * TB, AK + c * NTOK + (b + 1) * TB))
        if stage <= 2:
            return self.ffn_dummy_mixer(li)
        bkv = self.wload(wqkv_d[j, 5], KC * 512)
        for tt_ in range(12):
            prompt = tt_ < 4
            ps = self.ps_next()
            c0 = 0 if prompt else 256
            for kc in range(KC):
                self.mm(ps.v(c0, 512), self.H.v(kc * NTOK + tt_ * 128, kc * NTOK + (tt_ + 1) * 128),
                        self.WR.v(bkv + kc * 512 + c0, bkv + kc * 512 + 512), kc == 0, kc == KC - 1)
            for g in range(4):
                if self.cfg.get("s3_nocopy"):
                    break
                self.copy_any(A.vb(AVP + tt_ * 512 + g * 128 + (g % 2) * 64, AVP + tt_ * 512 + g * 128 + (g % 2) * 64 + 64),
                              ps.v(256 + g * 64, 256 + g * 64 + 64))
            if prompt and not self.cfg.get("s3_nodma"):
                si = 2
                stg = A.v(ASTG + si * 512, ASTG + (si + 1) * 512)
                self.act(stg, ps.v(0, 512), AF.Identity)
                S.dma("sp", nk_d[j, tt_ * 128:(tt_ + 1) * 128, :], A.v(ASTG + si * 512, ASTG + si * 512 + 256), slot="ok")
                S.dma("sp", nv_d[j, tt_ * 128:(tt_ + 1) * 128, :], A.v(ASTG + si * 512 + 256, ASTG + si * 512 + 512), slot="ov")
        if stage <= 3:
            return self.ffn_dummy_mixer(li)
        self.S.tag = "L%d attn core" % li
        self.attn_units = []
        for sq in range(2):
            for i in range(2):
                for g in range(4):
                    self.attn_core(sq * 2 + i, g, [("loc", sq * 2, None), ("loc", sq * 2 + 1, None)])
        for i in range(8):
            for g in range(4):
                kts = []
                if i > 0:
                    kts.append(("loc", 4 + i - 1, 0))
                kts.append(("loc", 4 + i, None))
                if i < 7:
                    kts.append(("loc", 4 + i + 1, 1))
                kts += [("ctx", 0, None), ("ctx", 1, None)]
                self.attn_core(4 + i, g, kts)
        self.attn_flush()
        if stage <= 4:
            return self.ffn_dummy_mixer(li)
        self.stat_base = APT
        self.S.tag = "L%d attn oproj" % li
        for pi in range(2):
            bw = self.wload(wo_d[j, pi], KC * 512)
            for b in range(NB):
                for mi in range(4):
                    m = pi * 4 + mi
                    ps = self.ps_next(pool=(0, 1))
                    for c in range(8):
                        self.mm(ps.v(0, 512), self.WR.v(bw + c * 512 + mi * 128, bw + c * 512 + (mi + 1) * 128),
                                A.vb(AQ + c * NTOK + b * TB, AQ + c * NTOK + (b + 1) * TB), c == 0, c == 7)
                    n = 0 if b == 0 else 1
                    cidx = (5 * 8 + m) * 2 + n
                    self.resid_and_stats(ps, m, b, self.mod1v(cidx, cidx + 1))
        self.S.tag = "L%d ln1" % li
        self.ln(li, 1, next_k=6)

    def attn_core(self, qt, g, keytiles):
        self.attn_units.append((qt, g, keytiles))

    def attn_flush(self):
        A = self.A
        units = self.attn_units
        self.attn_units = []
        steps = []
        for ui, (qt, g, kts) in enumerate(units):
            for ki, kt in enumerate(kts):
                steps.append((ui, ki, len(kts), kt))
        unit_ps = {}
        state = {}

        def stage_a(si):
            ui, ki, n, (kind, idx, mask) = steps[si]
            qt, g, _ = units[ui]
            half = g % 2
            p0, p1 = half * 64, half * 64 + 64
            kc_ = g // 2
            qc0 = (g // 2) * 4
            if ki == 0:
                unit_ps[ui] = (self.PS[2 + ui % 6], None)
            psS = self.ps_next(pool=(0, 1))
            if kind == "loc":
                kview = A.vb(AK + kc_ * NTOK + idx * 128, AK + kc_ * NTOK + (idx + 1) * 128, p0, p1)
            else:
                kview = A.vb(AKC + kc_ * 256 + idx * 128, AKC + kc_ * 256 + (idx + 1) * 128, p0, p1)
            if mask is not None:
                self.mm(psS.v(0, 512), self.IDB.v(0, 128), self.MASKS.v(mask * 512, (mask + 1) * 512), True, False)
            qkeys = []
            for jj in range(4):
                qkeys += A.vb(AQ + (qc0 + jj) * NTOK + qt * 128, AQ + (qc0 + jj) * NTOK + (qt + 1) * 128, p0, p1).keys
            qap = A.h[p0:p1, AQ // 2:(AQ + 8 * NTOK) // 2].bitcast(BF16).rearrange("p (c t) -> p c t", c=8)[
                :, qc0:qc0 + 4, qt * 128:(qt + 1) * 128]
            self.mm(psS.v(0, 512), kview, View(qap, qkeys), mask is None, True)
            pt = A.vb(APT + (self.pt_i % NPT) * 512, APT + (self.pt_i % NPT + 1) * 512)
            self.pt_i += 1
            self.act(pt, psS.v(0, 512), AF.Exp, scale=0.125)
            state[si] = pt

        def stage_c(si):
            ui, ki, n, (kind, idx, mask) = steps[si]
            qt, g, _ = units[ui]
            psO, psL = unit_ps[ui]
            pt = state.pop(si)
            vbase = AVP + idx * 512 if kind == "loc" else AVP + (12 + idx) * 512
            self.mm(psO.v(0, 512), A.vb(vbase + g * 128, vbase + (g + 1) * 128), pt, ki == 0, ki == n - 1)
            if ki == n - 1:
                self.attn_finalize(qt, g, psO, psL)

        ns = len(steps)
        DEPTH_A = 1
        for si in range(ns + DEPTH_A):
            if si < ns:
                stage_a(si)
            if si >= DEPTH_A:
                stage_c(si - DEPTH_A)

    def attn_finalize(self, qt, g, psO, psL):
        A = self.A
        half = g % 2
        p0, p1 = half * 64, half * 64 + 64
        qc0 = (g // 2) * 4
        rt = A.v(ART + (self.rt_i % 2) * 512, ART + (self.rt_i % 2 + 1) * 512, p0, p1)
        self.rt_i += 1
        rt3 = rt.ap.rearrange("p (h q) -> p h q", h=4)
        q0, q1 = (64, 128) if half == 0 else (0, 64)
        l3 = psO.v(0, 512, q0, q1).ap.rearrange("p (h q) -> p h q", h=4)
        es = self.ESINK.v(g * 4, g * 4 + 4, p0, p1)
        es3 = es.ap.unsqueeze(2).broadcast_to([64, 4, 128])
        self.S.op("dve", lambda e, o=rt3, a=l3, b=es3: e.tensor_tensor(o, a, b, ALU.add),
                  reads=[psO.v(0, 512), es], writes=[rt])
        self.recip(rt)
        keys = []
        for jj in range(4):
            keys += A.vb(AQ + (qc0 + jj) * NTOK + qt * 128, AQ + (qc0 + jj) * NTOK + (qt + 1) * 128, p0, p1).keys
        oap = A.h[p0:p1, AQ // 2:(AQ + 8 * NTOK) // 2].bitcast(BF16).rearrange("p (c t) -> p c t", c=8)[
            :, qc0:qc0 + 4, qt * 128:(qt + 1) * 128]
        ov = View(oap, keys)
        o3 = psO.v(0, 512, p0, p1).ap.rearrange("p (h q) -> p h q", h=4)
        self.S.op("dve", lambda e, o=oap, a=o3, b=rt3: e.tensor_tensor(o, a, b, ALU.mult),
                  reads=[psO.v(0, 512), rt], writes=[ov])

    def pool_ts(self, out, in0, s1, s2, op0, op1):
        rd = [in0] + [x for x in (s1, s2) if isinstance(x, View)]
        a1 = s1.ap if isinstance(s1, View) else s1
        a2 = s2.ap if isinstance(s2, View) else s2
        self.S.op("pool", lambda e, o=out.ap, a=in0.ap: e.tensor_scalar(o, a, a1, a2, op0, op1), reads=rd, writes=[out])

    def sin_reduce(self, zs, npart):
        MAGIC = 12582912.0
        t = self.tmp_next()
        tv = View(t.ap[0:npart, 0:zs.ap.shape[1]], t.keys)
        self.ts(tv, zs, 1.0 / (2.0 * math.pi), MAGIC, ALU.mult, ALU.add)
        self.ts(tv, tv, MAGIC, -2.0 * math.pi, ALU.subtract, ALU.mult)
        self.tt(zs, zs, tv, ALU.add)
        self.ts(zs, zs, -math.pi, math.pi, ALU.max, ALU.min)

    def hy_filter_mlp(self, j):
        A, S = self.A, self.S
        S.dma("sp", self.LNA.v(0, 1280, 0, 33), self.hy_d["feats"][0:33, :], slot="feats")
        for (c0, c1) in ((0, 512), (512, 1024), (1024, 1280)):
            n = c1 - c0
            ps = self.ps_next()
            self.mm(ps.v(0, n, 0, 64), self.FW1.v(j * 64, j * 64 + 64, 0, 33), self.LNA.v(c0, c1, 0, 33), True, True)
            zs = self.LNB.v(c0, c1, 0, 64)
            self.act(zs, ps.v(0, n, 0, 64), AF.Identity, bias=self.FBT.v(j * 2, j * 2 + 1, 0, 64))
            self.sin_reduce(zs, 64)
            self.act(zs, zs, AF.Sin)
        for (c0, c1) in ((0, 512), (512, 1024), (1024, 1280)):
            n = c1 - c0
            ps = self.ps_next()
            self.mm(ps.v(0, n, 0, 64), self.FW2.v(j * 64, j * 64 + 64, 0, 64), self.LNB.v(c0, c1, 0, 64), True, True)
            zs = self.LNA.v(c0, c1, 0, 64)
            self.act(zs, ps.v(0, n, 0, 64), AF.Identity, bias=self.FBT.v(j * 2 + 1, j * 2 + 2, 0, 64))
            self.sin_reduce(zs, 64)
            self.act(A.vb(HA2 + c0, HA2 + c1, 0, 64), zs, AF.Sin)

    def uview(self, cc, c0, c1):
        t = self.LNA if cc < 2 else self.LNB
        base = (cc % 2) * 512
        return t.v(base + c0, base + c1)

    def hy_inproj(self, j, chunk, b, wbase, cc):
        woff = (chunk % 4) * 128
        ps = self.ps_next(pool=(0, 1, 2, 3, 4, 5, 6))
        for kc in range(KC):
            self.mm(ps.v(0, 512), self.WR.v(wbase + kc * 512 + woff, wbase + kc * 512 + woff + 128),
                    self.H.v(kc * NTOK + b * TB, kc * NTOK + (b + 1) * TB), kc == 0, kc == KC - 1)
        psh = None
        if b > 0:
            tokh = 1024 if b == 1 else 1023
            psh = self.ps_next(pool=(0, 1, 2, 3, 4, 5, 6))
            for kc in range(KC):
                self.mm(psh.v(0, 1), self.WR.v(wbase + kc * 512 + woff, wbase + kc * 512 + woff + 128),
                        self.H.v(kc * NTOK + tokh, kc * NTOK + tokh + 1), kc == 0, kc == KC - 1)
        cp = (j * 24 + chunk) * 4
        w0, w1, w2, bb = (self.CONVP.v(cp + i, cp + i + 1) for i in range(4))
        self.act(self.uview(cc, 0, 512), ps.v(0, 512), AF.Identity, bias=bb, scale=w1)
        segs = [(0, 256), (256, 512)] if b == 0 else [(0, 512)]
        for (s0, e0) in segs:
            self.stt(self.uview(cc, s0 + 1, e0), ps.v(s0, e0 - 1), w0, self.uview(cc, s0 + 1, e0), ALU.mult, ALU.add)
            self.stt(self.uview(cc, s0, e0 - 1), ps.v(s0 + 1, e0), w2, self.uview(cc, s0, e0 - 1), ALU.mult, ALU.add)
        if b == 1:
            self.stt(self.uview(cc, 511, 512), psh.v(0, 1), w2, self.uview(cc, 511, 512), ALU.mult, ALU.add)
        if b == 2:
            self.stt(self.uview(cc, 0, 1), psh.v(0, 1), w0, self.uview(cc, 0, 1), ALU.mult, ALU.add)

    def hy_to_tokmajor(self, src, b, cc):
        A = self.A
        ps = self.ps_next(pool=(0, 1, 2, 3, 4, 5, 6))
        for i in range(4):
            sv = View(src.ap[:, i * 128:(i + 1) * 128], src.keys)
            self.S.op("pe", lambda e, o=ps.v(i * 128, (i + 1) * 128).ap, s_=sv.ap, idn=self.IDENT.h[:, :]: e.transpose(o, s_, idn),
                      reads=[sv, self.IDENT.v(0, 128)], writes=[ps.v(i * 128, (i + 1) * 128)])
        keys = []
        for i in range(4):
            keys += A.vb(HVZ + (b * 4 + i) * 512 + cc * 128, HVZ + (b * 4 + i) * 512 + (cc + 1) * 128).keys
        dap = A.h[:, HVZ // 2:(HVZ + 12 * 512) // 2].bitcast(BF16).rearrange("p (k c) -> p k c", k=12)[
            :, b * 4:b * 4 + 4, cc * 128:(cc + 1) * 128]
        dst = View(dap, keys)
        p3 = ps.v(0, 512).ap.rearrange("p (k c) -> p k c", k=4)
        self.S.op("act", lambda e, o=dap, i_=p3: e.activation(o, i_, AF.Identity), reads=[ps.v(0, 512)], writes=[dst])

    def hy_taps(self, j, order, cb, L):
        A, S = self.A, self.S
        KT = L // 128
        a2off = HA2 + (0 if L == 1024 else 1024)
        dec_d = self.hy_d["dec1024"] if L == 1024 else self.hy_d["dec256"]
        w3v = A.vb(HW3, HW3 + 1024, 0, 64)
        S.dma("pool", w3v, self.hy_d["w3p"][j, order, cb], slot="w3p")
        psN = self.PS[7]
        for kt in range(KT):
            dec = self.tmp_next()
            S.dma("sp", dec, dec_d[kt, :, cb * 512:(cb + 1) * 512], slot="dec%d" % (self.tmp_i % (self.NTMP - 2)))
            lhs = A.vb(a2off + kt * 128, a2off + (kt + 1) * 128, 0, 64)
            psf = self.ps_next(pool=(0, 1, 2, 3, 4, 5, 6))
            psb = self.ps_next(pool=(0, 1, 2, 3, 4, 5, 6))
            self.mm(psf.v(0, 512), lhs, A.vb(HW3, HW3 + 512, 0, 64), True, True)
            self.mm(psb.v(0, 512), lhs, A.vb(HW3 + 512, HW3 + 1024, 0, 64), True, True)
            tf = self.tmp_next()
            tb_ = self.tmp_next()
            self.tt(tf, psf.v(0, 512), dec, ALU.mult)
            self.tt(tb_, psb.v(0, 512), dec, ALU.mult)
            if kt == 0:
                z = View(tb_.ap[0:1, :], tb_.keys)
                S.op("dve", lambda e, o=z.ap: e.memset(o, 0.0), reads=[tb_], writes=[tb_])
            self.tt(A.vb(HTAPS + kt * 512, HTAPS + (kt + 1) * 512), tf, tb_, ALU.add)
            self.tt(A.vb(HTAPS + (KT + kt) * 512, HTAPS + (KT + kt + 1) * 512), tf, tb_, ALU.subtract)
            for ti, tsrc in enumerate((tf, tb_)):
                ab = self.ABSB.v((self.abs_i % 2) * 512, (self.abs_i % 2 + 1) * 512)
                self.abs_i += 1
                self.act(ab, tsrc, AF.Abs)
                self.mm(psN.v(0, 512), self.ONESK.v(0, 128), ab, kt == 0 and ti == 0, kt == KT - 1 and ti == 1)
        rn = self.TMP.v((self.NTMP - 2) * 512, (self.NTMP - 1) * 512)
        dn = self.TMP.v((self.NTMP - 1) * 512, self.NTMP * 512)
        self.ts(rn, psN.v(0, 512), 1e-6, None, ALU.add)
        S.dma("sp", dn, self.hy_d["hyd"][:, (j * 2 + order) * 1024 + cb * 512:(j * 2 + order) * 1024 + (cb + 1) * 512],
              slot="dbc")
        self.tt(dn, dn, rn, ALU.mult)
        self.recip(rn)
        ps = self.ps_next(pool=(0, 1, 2, 3, 4, 5, 6))
        for cc in range(4):
            src = View(rn.ap[0:1, cc * 128:(cc + 1) * 128], rn.keys)
            S.op("pe", lambda e, o=ps.v(cc, cc + 1).ap, s_=src.ap, idn=self.IDENT.h[0:1, 0:1]: e.transpose(o, s_, idn),
                 reads=[src, self.IDENT.v(0, 128)], writes=[ps.v(cc, cc + 1)])
        self.act(self.RNC.v(0, 4), ps.v(0, 4), AF.Identity)
        return rn, dn

    def hy_conv_group(self, j, order, cb, grp):
        A, S = self.A, self.S
        L = 256 if grp == "p" else 1024
        KT = L // 128
        self.S.tag = "taps o%d cb%d %s" % (order, cb, grp)
        rn, dn = self.hy_taps(j, order, cb, L)
        self.ckpt("taps_%d_%d_%s" % (order, cb, grp))
        self.S.tag = "fwd o%d cb%d %s" % (order, cb, grp)
        nseq = 2 if grp == "p" else 1
        pool7 = (0, 1, 2, 3, 4, 5, 6)
        if grp == "s":
            fblocks = [(0, 4), (4, 8)]
        else:
            fblocks = [(0, 2)]
            bdft = self.wload(self.hy_d["dft256"], 2048)
        for fbi, (m0, m1) in enumerate(fblocks):
            if grp == "s":
                bC = self.wload(self.hy_d["dft1024"][0 * 2 + fbi], 4096)
                bS = self.wload(self.hy_d["dft1024"][1 * 2 + fbi], 4096)
                cstride = 512
            for mf in range(m0, m1):
                def ct(kt, which):
                    if grp == "s":
                        base = bC if which == 0 else bS
                        o = base + kt * 512 + (mf - m0) * 128
                    else:
                        o = bdft + which * 512 + kt * 256 + mf * 128
                    return self.WR.v(o, o + 128)
                psGr = self.ps_next(pool=pool7)
                psGi = self.ps_next(pool=pool7)
                for kt in range(KT):
                    self.mm(psGr.v(0, 512), ct(kt, 0), A.vb(HTAPS + kt * 512, HTAPS + (kt + 1) * 512), kt == 0, kt == KT - 1)
                for kt in range(KT):
                    self.mm(psGi.v(0, 512), ct(kt, 1), A.vb(HTAPS + (KT + kt) * 512, HTAPS + (KT + kt + 1) * 512), kt == 0, kt == KT - 1)
                gr = A.vb(HG + mf * 512, HG + (mf + 1) * 512)
                gi = A.vb(HG + (KT + mf) * 512, HG + (KT + mf + 1) * 512)
                self.tt(gr, psGr.v(0, 512), dn, ALU.add)
                self.act(gi, psGi.v(0, 512), AF.Identity)
                for sq in range(nseq):
                    ktb = (4 if grp == "s" else sq * 2)
                    psZr = self.ps_next(pool=pool7)
                    psZi = self.ps_next(pool=pool7)
                    for kt in range(KT):
                        self.mm(psZr.v(0, 512), ct(kt, 0), A.vb(HVZ + (ktb + kt) * 512, HVZ + (ktb + kt + 1) * 512), kt == 0, kt == KT - 1)
                    for kt in range(KT):
                        self.mm(psZi.v(0, 512), ct(kt, 1), A.vb(HVZ + (ktb + kt) * 512, HVZ + (ktb + kt + 1) * 512), kt == 0, kt == KT - 1)
                    ta, tb_, tc, td = (self.tmp_next() for _ in range(4))
                    self.tt(ta, psZr.v(0, 512), gr, ALU.mult)
                    self.tt(tc, psZr.v(0, 512), gi, ALU.mult)
                    self.tt(tb_, psZi.v(0, 512), gi, ALU.mult)
                    self.tt(td, psZi.v(0, 512), gr, ALU.mult)
                    if grp == "s":
                        yr, yi = gr, gi
                    else:
                        yb = HG + 2048 + sq * 2048
                        yr = A.vb(yb + mf * 512, yb + (mf + 1) * 512)
                        yi = A.vb(yb + (2 + mf) * 512, yb + (3 + mf) * 512)
                    self.tt(yr, ta, tb_, ALU.subtract)
                    self.tt(yi, tc, td, ALU.add)
        self.ckpt("fwd_%d_%d_%s" % (order, cb, grp))
        self.S.tag = "inv o%d cb%d %s" % (order, cb, grp)
        xpart = 1 + order
        blocks = [0] if grp == "p" else [1, 2]
        for b in blocks:
            bw = self.wload(self.hy_d["win"][j, xpart * 2 + cb], KC * 512)
            for cc in range(4):
                self.hy_inproj(j, xpart * 8 + cb * 4 + cc, b, bw, cc)
            if grp == "s":
                tbi = b - 1
                bC = self.wload(self.hy_d["dft1024"][2 * 2 + tbi], 4096)
                bS = self.wload(self.hy_d["dft1024"][3 * 2 + tbi], 4096)
            for cc in range(4):
                ps = self.ps_next(pool=pool7)
                if grp == "s":
                    for mf in range(8):
                        self.mm(ps.v(0, 512), A.vb(HG + mf * 512 + cc * 128, HG + mf * 512 + (cc + 1) * 128),
                                self.WR.v(bC + mf * 512, bC + (mf + 1) * 512), mf == 0, False)
                    for mf in range(8):
                        self.mm(ps.v(0, 512), A.vb(HG + (8 + mf) * 512 + cc * 128, HG + (8 + mf) * 512 + (cc + 1) * 128),
                                self.WR.v(bS + mf * 512, bS + (mf + 1) * 512), False, mf == 7)
                else:
                    for sq in range(2):
                        yb = HG + 2048 + sq * 2048
                        for mf in range(2):
                            self.mm(ps.v(sq * 256, (sq + 1) * 256), A.vb(yb + mf * 512 + cc * 128, yb + mf * 512 + (cc + 1) * 128),
                                    self.WR.v(bdft + 1024 + mf * 256, bdft + 1024 + (mf + 1) * 256), mf == 0, False)
                        for mf in range(2):
                            self.mm(ps.v(sq * 256, (sq + 1) * 256), A.vb(yb + (2 + mf) * 512 + cc * 128, yb + (2 + mf) * 512 + (cc + 1) * 128),
                                    self.WR.v(bdft + 1536 + mf * 256, bdft + 1536 + (mf + 1) * 256), False, mf == 1)
                rnc = self.RNC.v(cc, cc + 1)
                if order == 0:
                    t = self.tmp_next()
                    self.stt(t, self.uview(cc, 0, 512), rnc, ps.v(0, 512), ALU.mult, ALU.mult)
                    self.hy_to_tokmajor(t, b, cc)
                else:
                    ch = cb * 4 + cc
                    self.stt(A.vb(HZ2 + ch * NTOK + b * TB, HZ2 + ch * NTOK + (b + 1) * TB), self.uview(cc, 0, 512), rnc,
                             ps.v(0, 512), ALU.mult, ALU.mult)

    def hyena(self, li):
        j = li // 2
        A, S = self.A, self.S
        self.S.tag = "filter_mlp"
        self.hy_filter_mlp(j)
        for cb in range(2):
            self.S.tag = "vbranch cb%d" % cb
            for b in range(NB):
                bw = self.wload(self.hy_d["win"][j, cb], KC * 512)
                for cc in range(4):
                    self.hy_inproj(j, cb * 4 + cc, b, bw, cc)
                for cc in range(4):
                    self.hy_to_tokmajor(self.uview(cc, 0, 512), b, cc)
            self.ckpt("vbranch_%d" % cb)
            for order in range(2):
                for grp in ("p", "s"):
                    self.hy_conv_group(j, order, cb, grp)
                    self.ckpt("conv_%d_%d_%s" % (order, cb, grp))
        self.stat_base = HTAPS
        self.S.tag = "hy oproj"
        for pi in range(2):
            bw = self.wload(self.hy_d["wout"][j, pi], KC * 512)
            for b in range(NB):
                for mi in range(4):
                    m = pi * 4 + mi
                    ps = self.ps_next(pool=(0, 1))
                    for c in range(8):
                        self.mm(ps.v(0, 512), self.WR.v(bw + c * 512 + mi * 128, bw + c * 512 + (mi + 1) * 128),
                                A.vb(HZ2 + c * NTOK + b * TB, HZ2 + c * NTOK + (b + 1) * TB), c == 0, c == 7)
                    n = 0 if b == 0 else 1
                    cidx = (5 * 8 + m) * 2 + n
                    self.resid_and_stats(ps, m, b, self.mod1v(cidx, cidx + 1))
        self.S.tag = "L%d ln1" % li
        self.ln(li, 1, next_k=6)


def _layout_common(inp):
    f = np.float32
    out = {}
    ada_w = np.asarray(inp["ada_w"], f)
    out["adaw"] = np.ascontiguousarray(
        ada_w.reshape(DEPTH, KC, 128, 18, 512).transpose(0, 3, 2, 1, 4).reshape(DEPTH, 18, 128, KC * 512))
    ada_b = np.asarray(inp["ada_b"], f)
    out["adab"] = np.ascontiguousarray(ada_b.reshape(DEPTH, 72, 128).transpose(2, 0, 1).reshape(128, DEPTH * 72))
    for nm, key in (("lng", "ln_g"), ("lnb", "ln_b")):
        a = np.asarray(inp[key], f)
        out[nm] = np.ascontiguousarray(a.reshape(DEPTH * 3, KC, 128).transpose(2, 0, 1).reshape(128, DEPTH * 3 * KC))
    w1 = np.asarray(inp["ffn_w1"], f).reshape(DEPTH * 2, KC, 128, 2, 11, 2, 128)
    out["w1"] = np.ascontiguousarray(w1.transpose(0, 4, 2, 1, 5, 3, 6).reshape(DEPTH * 2, 11, 128, KC * 512))
    w2 = np.asarray(inp["ffn_w2"], f).reshape(DEPTH * 2, NJ, 128, KC, 128)
    out["w2"] = np.ascontiguousarray(w2.transpose(0, 3, 2, 1, 4).reshape(DEPTH * 2, KC, 128, NJ * 128))
    part = np.array([d + 16 if (d % 32) < 16 else d - 16 for d in range(64)])
    qcols, qpcols = [], []
    for cch in range(8):
        for hf in range(2):
            g = (cch // 4) * 2 + hf
            h = g * 4 + (cch % 4)
            qcols += [h * 64 + d for d in range(64)]
            qpcols += [h * 64 + int(part[d]) for d in range(64)]
    kcols = [1024 + g * 64 + d for g in range(4) for d in range(64)]
    kpcols = [1024 + g * 64 + int(part[d]) for g in range(4) for d in range(64)]
    wq = np.asarray(inp["attn_w_qkv"], f)
    pieces = [wq[:, :, qcols[0:512]], wq[:, :, qcols[512:1024]], wq[:, :, qpcols[0:512]], wq[:, :, qpcols[512:1024]],
              wq[:, :, kcols + kpcols], wq[:, :, 1024:1536]]
    wl = np.stack(pieces, axis=1)
    out["wqkv"] = np.ascontiguousarray(wl.reshape(2, 6, KC, 128, 512).transpose(0, 1, 3, 2, 4).reshape(2, 6, 128, KC * 512))
    wo = np.asarray(inp["attn_w_o"], f)[:, qcols, :]
    out["wo"] = np.ascontiguousarray(wo.reshape(2, KC, 128, 2, 512).transpose(0, 3, 2, 1, 4).reshape(2, 2, 128, KC * 512))
    t = np.arange(1024)
    pos = np.stack([t // 64, t % 64], 0).astype(np.float64)
    inv = 10000.0 ** (-np.arange(16, dtype=np.float64) / 16)
    rope = np.zeros((128, 2048), np.float64)
    for p in range(128):
        d = p % 64
        ang = pos[d // 32] * inv[d % 16]
        rope[p, :1024] = np.cos(ang)
        rope[p, 1024:] = np.sin(ang) * (-1.0 if (d % 32) < 16 else 1.0)
    out["rope"] = rope.astype(f)
    kk = np.arange(128)[:, None]
    qq = np.arange(128)[None, :]
    m_lo = np.where(qq <= kk, 0.0, -30000.0)
    m_hi = np.where(kk <= qq, 0.0, -30000.0)
    out["masks"] = np.concatenate([np.tile(m_lo, (1, 4)), np.tile(m_hi, (1, 4))], axis=1).astype(f)
    out["ident"] = np.eye(128, dtype=f)
    sk = np.asarray(inp["attn_sink"], f).reshape(1, 32)
    out["sinkb"] = np.ascontiguousarray(np.broadcast_to(sk, (128, 32)))
    hw = np.asarray(inp["hy_w_in"], f)
    out["hwin"] = np.ascontiguousarray(hw.reshape(2, KC, 128, 6, 512).transpose(0, 3, 2, 1, 4).reshape(2, 6, 128, KC * 512))
    ho = np.asarray(inp["hy_w_out"], f)
    out["hwout"] = np.ascontiguousarray(ho.reshape(2, KC, 128, 2, 512).transpose(0, 3, 2, 1, 4).reshape(2, 2, 128, KC * 512))
    w3 = np.asarray(inp["hy_f_w3"], f).reshape(2, 64, 2, 2, 2, 512)
    out["w3p"] = np.ascontiguousarray(w3.transpose(0, 2, 4, 1, 3, 5).reshape(2, 2, 2, 64, 1024))
    hd = np.asarray(inp["hy_d"], f).reshape(1, 4096)
    out["hyd"] = np.ascontiguousarray(np.broadcast_to(hd, (128, 4096)))
    cw = np.asarray(inp["hy_conv_w"], f)
    cbias = np.asarray(inp["hy_conv_b"], f)
    cp = np.concatenate([cw, cbias[:, None, :]], axis=1)
    out["convp"] = np.ascontiguousarray(cp.reshape(2, 4, 24, 128).transpose(3, 0, 2, 1).reshape(128, 2 * 24 * 4))
    fw1 = np.zeros((128, 128), f)
    fw1[:33, :] = np.asarray(inp["hy_f_w1"], f).transpose(1, 0, 2).reshape(33, 128)
    out["fw1"] = fw1
    fw2 = np.zeros((128, 128), f)
    fw2[:64, :] = np.asarray(inp["hy_f_w2"], f).transpose(1, 0, 2).reshape(64, 128)
    out["fw2"] = fw2
    fb = np.zeros((128, 4), f)
    for jj in range(2):
        fb[:64, jj * 2] = np.asarray(inp["hy_f_b1"], f)[jj]
        fb[:64, jj * 2 + 1] = np.asarray(inp["hy_f_b2"], f)[jj]
    out["fb"] = fb
    out.update(_hyena_constants())
    return out


_HC = {}


def _hyena_constants():
    if _HC:
        return _HC
    f = np.float32
    feats = np.zeros((128, 1280), np.float64)
    col = 0
    for L in (1024, 256):
        t = np.arange(L, dtype=np.float64) / L
        bands = np.arange(1, 17, dtype=np.float64)
        ph = 2 * np.pi * t[:, None] * bands[None]
        ft = np.concatenate([t[:, None], np.sin(ph), np.cos(ph)], -1)
        feats[:33, col:col + L] = ft.T
        col += L
    _HC["feats"] = feats.astype(f)
    max_decay = math.log(1e-2) / 0.3
    min_decay = math.log(1e-2) / 1.5
    deltas = np.abs(np.linspace(min_decay, max_decay, 1024, dtype=np.float32)).astype(np.float64)
    for L in (1024, 256):
        t = (np.arange(L, dtype=np.float32) / L).astype(np.float64)
        dec = np.exp(-t[:, None] * deltas[None])
        _HC["dec%d" % L] = np.ascontiguousarray(dec.reshape(L // 128, 128, 1024)).astype(f)
    L = 1024
    ff = np.arange(L, dtype=np.float64)
    tt = np.arange(L, dtype=np.float64)
    th = np.pi * np.outer(tt, 2 * ff + 1) / (2 * L)
    Ct, nSt = np.cos(th), -np.sin(th)
    Cf, nSf = np.cos(th).T / L, -np.sin(th).T / L
    pcs = []
    for M in (Ct, nSt, Cf, nSf):
        for blk in range(2):
            pcs.append(M[:, blk * 512:(blk + 1) * 512].reshape(8, 128, 512).transpose(1, 0, 2).reshape(128, 4096))
    _HC["dft1024"] = np.ascontiguousarray(np.stack(pcs, 0)).astype(f)
    L = 256
    ff = np.arange(L, dtype=np.float64)
    tt = np.arange(L, dtype=np.float64)
    th = np.pi * np.outer(tt, 2 * ff + 1) / (2 * L)
    mats = [np.cos(th), -np.sin(th), np.cos(th).T / L, -np.sin(th).T / L]
    _HC["dft256"] = np.ascontiguousarray(np.concatenate(
        [M.reshape(2, 128, 256).transpose(1, 0, 2).reshape(128, 512) for M in mats], axis=1)).astype(f)
    return _HC


def _per_core(inp, c):
    f = np.float32
    xp = np.asarray(inp["x_prompt"], f)[2 * c:2 * c + 2].reshape(512, D)
    xs = np.asarray(inp["x_sample"], f)[c]
    x = np.concatenate([xp, xs], axis=0)
    m = {"xT": np.ascontiguousarray(x.T)}
    cond = np.stack([np.asarray(inp["c_ctx"], f), np.asarray(inp["c"], f)[c]], axis=0)
    m["condT"] = np.ascontiguousarray(cond.reshape(2, KC, 128).transpose(2, 1, 0).reshape(128, KC * 2))
    m["ck"] = np.ascontiguousarray(np.asarray(inp["cache_k"], f)[c].reshape(2, 256, 256))
    m["cv"] = np.ascontiguousarray(np.asarray(inp["cache_v"], f)[c].reshape(2, 256, 256))
    return m


def run(inp, cfg):
    common = _layout_common(inp)
    b = Builder(cfg)
    nc = b.build()
    in_maps = []
    for c in range(NCORES):
        m = dict(common)
        m.update(_per_core(inp, c))
        in_maps.append(m)
    res = run_bass_kernel_spmd(nc, in_maps, core_ids=list(range(NCORES)))
    b.stack.close()
    return res.results


def kernel(**inputs):
    cfg = {"mixers": "all"}
    results = run(inputs, cfg)
    return assemble(results)


def assemble(results):
    yp = np.zeros((16, 256, D), np.float32)
    ys = np.zeros((8, 1024, D), np.float32)
    for c in range(NCORES):
        y = np.ascontiguousarray(results[c]["yT"].T)
        yp[2 * c:2 * c + 2] = y[:512].reshape(2, 256, D)
        ys[c] = y[512:]
    nk = np.zeros((16, 2, 256, 4, 64), np.float32)
    nv = np.zeros((16, 2, 256, 4, 64), np.float32)
    for c in range(NCORES):
        k = results[c]["nk"].reshape(2, 2, 256, 4, 64)
        v = results[c]["nv"].reshape(2, 2, 256, 4, 64)
        nk[2 * c:2 * c + 2] = k.transpose(1, 0, 2, 3, 4)
        nv[2 * c:2 * c + 2] = v.transpose(1, 0, 2, 3, 4)
    return (yp, ys, nk, nv)
```
